# Optimizing a Trainium2 kernel written in Bass

```python
import jax, jax.numpy as jnp
from jax import lax
import numpy as np

D_MODEL = 1024
BATCH = 8
SEQ = 2048
DEPTH = 2

EPS = 1e-6
NEG_INF = -1e30
CONV_W = 3
CONV_CH = D_MODEL // 4
CONV_GROUPS = 4
DN_HEADS = 4
DN_HEAD_DIM = 128
DN_WIDTH = DN_HEADS * DN_HEAD_DIM
DN_CHUNK = 64
SWA_HEAD_DIM = 64
SWA_HEADS = 4
SWA_PATTERNS = ((128, 1), (512, 4), (2048, 16))
SWA_N_PAT = 3
SWA_WIDTH = SWA_HEADS * SWA_HEAD_DIM
SWA_QKV_WIDTH = SWA_N_PAT * SWA_WIDTH
SWA_BLOCK = 64
ROPE_THETA = 500000.0
ROPE_DIM = SWA_HEAD_DIM // 4
D_FF = 2816
MIX_WIDTH = CONV_CH + DN_WIDTH + SWA_WIDTH
IN_SIZES = (CONV_CH, CONV_CH, CONV_CH,
            DN_WIDTH, DN_WIDTH, DN_WIDTH, DN_WIDTH,
            DN_HEADS, DN_HEADS, DN_HEADS, DN_HEADS,
            SWA_QKV_WIDTH, SWA_QKV_WIDTH, SWA_QKV_WIDTH)
IN_WIDTH = sum(IN_SIZES)

kernel_name = 'hybrid_parallel_heads_encoder'


def rmsnorm(x, g):
    xf = x.astype(jnp.float32)
    y = xf * lax.rsqrt(jnp.mean(xf * xf, axis=-1, keepdims=True) + EPS)
    return (y * g.astype(jnp.float32)).astype(x.dtype)


def group_rmsnorm(x, g, n_groups):
    shp = x.shape
    xf = x.astype(jnp.float32).reshape(shp[:-1] + (n_groups, shp[-1] // n_groups))
    y = xf * lax.rsqrt(jnp.mean(xf * xf, axis=-1, keepdims=True) + EPS)
    return (y.reshape(shp) * g.astype(jnp.float32)).astype(x.dtype)


def dwconv3(x, w):
    xp = jnp.pad(x, ((0, 0), (1, 1), (0, 0)))
    return xp[:, :-2] * w[0] + xp[:, 1:-1] * w[1] + xp[:, 2:] * w[2]


def l2norm(x):
    return x * lax.rsqrt(jnp.sum(x * x, axis=-1, keepdims=True) + EPS)


def split_cols(a, sizes):
    idx = [int(i) for i in np.cumsum(sizes)[:-1]]
    return jnp.split(a, idx, axis=-1)


def short_conv_mixer(xa, gate_b, gate_c, w_conv):
    return gate_b * dwconv3(gate_c * xa, w_conv)


def gated_delta_chunked(q, k, v, g, beta):
    b, h, t, dk = q.shape
    dv = v.shape[-1]
    c = DN_CHUNK
    n = t // c

    def chunks(a):
        return a.reshape(a.shape[:2] + (n, c) + a.shape[3:])

    q, k, v, g, beta = (chunks(a) for a in (q, k, v, g, beta))
    g = jnp.cumsum(g, axis=-1)
    incl = jnp.tril(jnp.ones((c, c), dtype=bool))
    strict = jnp.tril(jnp.ones((c, c), dtype=bool), -1)
    diff = g[..., :, None] - g[..., None, :]
    decay = jnp.where(incl, jnp.exp(jnp.where(incl, diff, 0.0)), 0.0)
    kb = k * beta[..., None]
    a_mat = jnp.where(strict, jnp.einsum('bhncd,bhnsd->bhncs', kb, k) * decay, 0.0)
    t_mat = a_mat + jnp.eye(c, dtype=a_mat.dtype)
    rhs = jnp.concatenate([v * beta[..., None], kb * jnp.exp(g)[..., None]], axis=-1)
    sol = lax.linalg.triangular_solve(t_mat, rhs, left_side=True, lower=True, unit_diagonal=True)
    u, w = sol[..., :dv], sol[..., dv:]
    qk = jnp.where(incl, jnp.einsum('bhncd,bhnsd->bhncs', q, k) * decay, 0.0)
    q_dec = q * jnp.exp(g)[..., None]
    k_dec = k * jnp.exp(g[..., -1:] - g)[..., None]
    g_last = jnp.exp(g[..., -1])
    xs = tuple(jnp.moveaxis(a, 2, 0) for a in (u, w, qk, q_dec, k_dec, g_last))

    def step(state, inp):
        u_i, w_i, qk_i, qd_i, kd_i, gl_i = inp
        v_new = u_i - jnp.einsum('bhck,bhkv->bhcv', w_i, state)
        o_i = jnp.einsum('bhck,bhkv->bhcv', qd_i, state) + jnp.einsum('bhcs,bhsv->bhcv', qk_i, v_new)
        state = state * gl_i[..., None, None] + jnp.einsum('bhck,bhcv->bhkv', kd_i, v_new)
        return state, o_i

    s0 = jnp.zeros((b, h, dk, dv), jnp.float32)
    _, o = lax.scan(step, s0, xs)
    return jnp.moveaxis(o, 0, 2).reshape(b, h, t, dv)


def gated_deltanet_mixer(q, k, v, gate, b_f, b_b, a_f, a_b, w_conv,
                         a_log_f, a_log_b, dt_bias_f, dt_bias_b, norm_g):
    bsz, t, _ = q.shape
    qkv = jax.nn.silu(dwconv3(jnp.concatenate([q, k, v], axis=-1), w_conv)).astype(jnp.float32)
    q, k, v = jnp.split(qkv, 3, axis=-1)

    def heads(a):
        return a.reshape(bsz, t, DN_HEADS, DN_HEAD_DIM).transpose(0, 2, 1, 3)

    q = l2norm(heads(q)) * (DN_HEAD_DIM ** -0.5)
    k = l2norm(heads(k))
    v = heads(v)

    def decay_beta(a, bl, a_log, dt_bias):
        g = -jnp.exp(a_log.astype(jnp.float32)) * jax.nn.softplus(a.astype(jnp.float32) + dt_bias.astype(jnp.float32))
        beta = jax.nn.sigmoid(bl.astype(jnp.float32))
        return g.transpose(0, 2, 1), beta.transpose(0, 2, 1)

    g_f, beta_f = decay_beta(a_f, b_f, a_log_f, dt_bias_f)
    g_b, beta_b = decay_beta(a_b, b_b, a_log_b, dt_bias_b)
    flip = lambda a: jnp.flip(a, axis=2)
    o_f = gated_delta_chunked(q, k, v, g_f, beta_f)
    o_b = flip(gated_delta_chunked(flip(q), flip(k), flip(v), flip(g_b), flip(beta_b)))
    o = (o_f + o_b).transpose(0, 2, 1, 3)
    o = o * lax.rsqrt(jnp.mean(o * o, axis=-1, keepdims=True) + EPS) * norm_g.astype(jnp.float32)
    o = o * jax.nn.silu(gate.astype(jnp.float32).reshape(bsz, t, DN_HEADS, DN_HEAD_DIM))
    return o.reshape(bsz, t, DN_WIDTH).astype(gate.dtype)


def partial_rope(x, positions):
    half = ROPE_DIM // 2
    inv_freq = ROPE_THETA ** (-jnp.arange(half, dtype=jnp.float32) / half)
    ang = positions.astype(jnp.float32)[:, :, None] * inv_freq
    cos = jnp.cos(ang)[:, :, None, None, :]
    sin = jnp.sin(ang)[:, :, None, None, :]
    xr = x[..., :ROPE_DIM].astype(jnp.float32)
    x1, x2 = xr[..., :half], xr[..., half:]
    rot = jnp.concatenate([x1 * cos - x2 * sin, x2 * cos + x1 * sin], axis=-1).astype(x.dtype)
    return jnp.concatenate([rot, x[..., ROPE_DIM:]], axis=-1)


def banded_attention(q, k, v, radius):
    lead = q.shape[:-2]
    L, dh = q.shape[-2], q.shape[-1]
    blk = SWA_BLOCK
    nb = -(-L // blk)
    pad = nb * blk - L
    nlead = len(lead)
    qb = jnp.pad(q, [(0, 0)] * nlead + [(0, pad), (0, 0)]).reshape(lead + (nb, blk, dh))

    def windows(a):
        ap = jnp.pad(a, [(0, 0)] * nlead + [(blk, pad + blk), (0, 0)]).reshape(lead + (nb + 2, blk, dh))
        return jnp.concatenate([ap[..., :-2, :, :], ap[..., 1:-1, :, :], ap[..., 2:, :, :]], axis=-2)

    kw, vw = windows(k), windows(v)
    q_pos = jnp.arange(nb)[:, None] * blk + jnp.arange(blk)[None, :]
    k_pos = jnp.arange(nb)[:, None] * blk - blk + jnp.arange(3 * blk)[None, :]
    kp = k_pos[:, None, :]
    valid = (jnp.abs(q_pos[:, :, None] - kp) <= radius) & (kp >= 0) & (kp < L)
    s = jnp.einsum('...nqd,...nkd->...nqk', qb, kw).astype(jnp.float32) * (dh ** -0.5)
    s = jnp.where(valid, s, NEG_INF)
    m = jnp.max(s, axis=-1, keepdims=True)
    p = jnp.exp(s - m)
    den = jnp.sum(p, axis=-1, keepdims=True)
    o = jnp.einsum('...nqk,...nkd->...nqd', p / den, vw.astype(jnp.float32))
    lse = (m + jnp.log(den))[..., 0]
    o = o.reshape(lead + (nb * blk, dh))[..., :L, :]
    lse = lse.reshape(lead + (nb * blk,))[..., :L]
    return o, lse


def dilated_window_attention(q, k, v, positions):
    b, t, _ = q.shape
    shp = (b, t, SWA_N_PAT, SWA_HEADS, SWA_HEAD_DIM)
    q = partial_rope(q.reshape(shp), positions)
    k = partial_rope(k.reshape(shp), positions)
    v = v.reshape(shp)
    outs, lses = [], []
    for p_idx, (window, dil) in enumerate(SWA_PATTERNS):
        L = t // dil
        radius = window // (2 * dil)

        def to_sub(a):
            return a.reshape(b, L, dil, SWA_HEADS, SWA_HEAD_DIM).transpose(0, 3, 2, 1, 4)

        o, lse = banded_attention(to_sub(q[:, :, p_idx]), to_sub(k[:, :, p_idx]), to_sub(v[:, :, p_idx]), radius)
        outs.append(o.transpose(0, 3, 2, 1, 4).reshape(b, t, SWA_HEADS, SWA_HEAD_DIM))
        lses.append(lse.transpose(0, 3, 2, 1).reshape(b, t, SWA_HEADS))
    alpha = jax.nn.softmax(jnp.stack(lses, axis=0), axis=0)
    o = jnp.einsum('pbth,pbthd->bthd', alpha, jnp.stack(outs, axis=0))
    return o.reshape(b, t, SWA_WIDTH).astype(q.dtype)


def conv_gated_mlp(h, w_up, w_conv, w_down):
    u = dwconv3(h @ w_up, w_conv)
    gate, val = jnp.split(u, 2, axis=-1)
    return (jax.nn.silu(gate) * val) @ w_down


def setup_inputs(seed: int = 0) -> dict:
    key = jax.random.key(seed)
    ks = jax.random.split(key, 24)
    f32 = jnp.float32

    def nrm(k, shape, scale):
        return jax.random.normal(k, shape, f32) * scale

    def gain(k, shape):
        return 1.0 + 0.02 * jax.random.normal(k, shape, f32)

    def dt_bias(k):
        dt = jnp.exp(jax.random.uniform(k, (DEPTH, DN_HEADS), f32, np.log(1e-3), np.log(1e-1)))
        return dt + jnp.log(-jnp.expm1(-dt))

    x = nrm(ks[0], (BATCH, SEQ, D_MODEL), 1.0)
    positions = (jax.random.randint(ks[1], (BATCH, 1), 0, 4096, dtype=jnp.int32)
                 + jnp.arange(SEQ, dtype=jnp.int32)[None, :])
    return {
        'x': x,
        'positions': positions,
        'norm_mix': gain(ks[2], (DEPTH, D_MODEL)),
        'w_in': nrm(ks[3], (DEPTH, D_MODEL, IN_WIDTH), D_MODEL ** -0.5),
        'conv_a': nrm(ks[4], (DEPTH, CONV_W, CONV_CH), CONV_W ** -0.5),
        'norm_a': gain(ks[5], (DEPTH, CONV_CH)),
        'conv_qkv': nrm(ks[6], (DEPTH, CONV_W, 3 * DN_WIDTH), CONV_W ** -0.5),
        'a_log_f': jnp.log(jax.random.uniform(ks[7], (DEPTH, DN_HEADS), f32, 1.0, 16.0)),
        'a_log_b': jnp.log(jax.random.uniform(ks[8], (DEPTH, DN_HEADS), f32, 1.0, 16.0)),
        'dt_bias_f': dt_bias(ks[9]),
        'dt_bias_b': dt_bias(ks[10]),
        'norm_dn': gain(ks[11], (DEPTH, DN_HEAD_DIM)),
        'norm_c': gain(ks[12], (DEPTH, SWA_WIDTH)),
        'w_o': nrm(ks[13], (DEPTH, MIX_WIDTH, D_MODEL), MIX_WIDTH ** -0.5),
        'norm_ffn': gain(ks[14], (DEPTH, D_MODEL)),
        'w_up': nrm(ks[15], (DEPTH, D_MODEL, 2 * D_FF), D_MODEL ** -0.5),
        'conv_ffn': nrm(ks[16], (DEPTH, CONV_W, 2 * D_FF), CONV_W ** -0.5),
        'w_down': nrm(ks[17], (DEPTH, D_FF, D_MODEL), D_FF ** -0.5),
        'norm_final': gain(ks[18], (D_MODEL,)),
    }


def reference(x, positions, norm_mix, w_in, conv_a, norm_a, conv_qkv, a_log_f, a_log_b,
              dt_bias_f, dt_bias_b, norm_dn, norm_c, w_o, norm_ffn, w_up, conv_ffn, w_down,
              norm_final):
    for l in range(DEPTH):
        h = rmsnorm(x, norm_mix[l])
        proj = h @ w_in[l]
        (xa, gb, gc, q_dn, k_dn, v_dn, gate_dn, b_f, b_b, a_f, a_b,
         q_c, k_c, v_c) = split_cols(proj, IN_SIZES)
        y_a = group_rmsnorm(short_conv_mixer(xa, gb, gc, conv_a[l]), norm_a[l], CONV_GROUPS)
        y_b = gated_deltanet_mixer(q_dn, k_dn, v_dn, gate_dn, b_f, b_b, a_f, a_b, conv_qkv[l],
                                   a_log_f[l], a_log_b[l], dt_bias_f[l], dt_bias_b[l], norm_dn[l])
        y_c = group_rmsnorm(dilated_window_attention(q_c, k_c, v_c, positions), norm_c[l], SWA_HEADS)
        x = x + jnp.concatenate([y_a, y_b, y_c], axis=-1) @ w_o[l]
        x = x + conv_gated_mlp(rmsnorm(x, norm_ffn[l]), w_up[l], conv_ffn[l], w_down[l])
    return rmsnorm(x, norm_final)
```

```python
import contextlib
import numpy as np
import concourse.bass as bass
import concourse.mybir as mybir
from concourse.bass_utils import run_bass_kernel_spmd

F32 = mybir.dt.float32
BF16 = mybir.dt.bfloat16
I32 = mybir.dt.int32
ALU = mybir.AluOpType
AF = mybir.ActivationFunctionType
AX = mybir.AxisListType

T = 2048
D = 1024
NT = 16
DEPTH = 2
DFF = 2816
NFT = 22
INW = 5136
EPS = 1e-6
C_XA, C_GB, C_GC = 0, 256, 512
C_QD, C_KD, C_VD, C_GD = 768, 1280, 1792, 2304
C_SC = 2816
C_QC, C_KC, C_VC = 2832, 3600, 4368
NPP = 192


class Prog:
    ENGS = ("pe", "act", "dve", "pool", "sp")
    EPOCH = 1000
    NDMA = 24

    def __init__(self, nc):
        self.nc = nc
        self.ins = []
        self.track = {}
        self.pstride = {}

    def _bbox(self, ap):
        t = ap.tensor
        name = t.name
        dims = ap.ap
        off = ap.offset
        isz = mybir.dt.size(ap.dtype)
        if "DRAM" in str(ap.space):
            lo = off
            hi = off + sum((c - 1) * abs(s) for s, c in dims) + 1
            return name, (0, 1, lo * isz, hi * isz)
        ps = int(np.prod(list(t.shape)[1:]))
        p0 = off // ps
        f0 = off % ps
        pstep, pcnt = dims[0]
        if pstep == 0:
            pcnt = 1
        f1 = f0 + sum((c - 1) * abs(s) for s, c in dims[1:]) + 1
        if "PSUM" in str(ap.space):
            return name, (0, 128, (f0 * isz) // 2048 * 2048, -((-f1 * isz) // 2048) * 2048)
        return name, (p0, p0 + pcnt, f0 * isz, f1 * isz)

    def add(self, eng, emit, reads=(), writes=(), dma=False):
        iid = len(self.ins)
        deps = set()
        rb = [self._bbox(a) for a in reads if "PSUM" not in str(a.space)]
        wb = [self._bbox(a) for a in writes] + [self._bbox(a) for a in reads if "PSUM" in str(a.space)]
        for name, bb in rb:
            for e in self.track.setdefault(name, []):
                if e[4] == "w" and e[0] < bb[1] and bb[0] < e[1] and e[2] < bb[3] and bb[2] < e[3]:
                    deps.add(e[5])
        for name, bb in wb:
            for e in self.track.setdefault(name, []):
                if e[0] < bb[1] and bb[0] < e[1] and e[2] < bb[3] and bb[2] < e[3]:
                    deps.add(e[5])
        for name, bb in wb:
            keep = []
            for e in self.track[name]:
                if bb[0] <= e[0] and e[1] <= bb[1] and bb[2] <= e[2] and e[3] <= bb[3]:
                    continue
                keep.append(e)
            keep.append([bb[0], bb[1], bb[2], bb[3], "w", iid])
            self.track[name] = keep
        for name, bb in rb:
            ents = self.track[name]
            found = False
            if not dma:
                for e in ents:
                    if e[4] == "r" and e[0] == bb[0] and e[1] == bb[1] and e[2] == bb[2] and e[3] == bb[3]:
                        if e[5] == iid:
                            found = True
                            break
                        o = self.ins[e[5]]
                        if o["eng"] == eng and not o["dma"]:
                            e[5] = iid
                            found = True
                            break
            if not found:
                ents.append([bb[0], bb[1], bb[2], bb[3], "r", iid])
        deps.discard(iid)
        self.ins.append(dict(eng=eng, emit=emit, deps=deps, dma=dma))
        return iid

    def emit_all(self):
        nc = self.nc
        ins = self.ins
        needed = set()
        for I in ins:
            nd = set()
            for d in I["deps"]:
                Dd = ins[d]
                if Dd["eng"] == "pe" and I["eng"] == "pe" and not Dd["dma"] and not I["dma"]:
                    continue
                nd.add(d)
            I["deps"] = nd
            needed |= nd
        cnt = {e: 0 for e in self.ENGS}
        dma_k = 0
        dma_val = [0] * self.NDMA
        for i, I in enumerate(ins):
            if I["dma"]:
                k = dma_k % self.NDMA
                dma_k += 1
                I["sig"] = ("dma", k, dma_val[k] + 16)
                I["prev"] = ("dma", k, dma_val[k])
                dma_val[k] += 16
            elif i in needed:
                c = cnt[I["eng"]]
                I["sig"] = (I["eng"], c // self.EPOCH, c % self.EPOCH + 1)
                cnt[I["eng"]] = c + 1
            else:
                I["sig"] = None
        print('signal counts', cnt, 'dmas', dma_k, 'per-engine instr', {e: sum(1 for I in ins if I['eng'] == e) for e in self.ENGS})
        nep = {e: cnt[e] // self.EPOCH + 1 for e in self.ENGS}
        with contextlib.ExitStack() as st:
            sems = {}
            for e in self.ENGS:
                for ep in range(nep[e]):
                    sems[(e, ep)] = st.enter_context(nc.semaphore(f"s_{e}_{ep}"))
            for k in range(min(self.NDMA, max(dma_k, 1))):
                sems[("dma", k)] = st.enter_context(nc.semaphore(f"s_dma_{k}"))
            block = st.enter_context(nc.Block())
            per_eng = {e: [I for I in ins if I["eng"] == e] for e in self.ENGS}

            def run(engname, engobj):
                waited = {}
                for I in per_eng[engname]:
                    ws = {}
                    for d in I["deps"]:
                        s = ins[d]["sig"]
                        key = (s[0], s[1])
                        if ws.get(key, 0) < s[2]:
                            ws[key] = s[2]
                    if I["dma"] and I["prev"][2] > 0:
                        p = I["prev"]
                        key = (p[0], p[1])
                        if ws.get(key, 0) < p[2]:
                            ws[key] = p[2]
                    for key, v in ws.items():
                        if waited.get(key, 0) >= v:
                            continue
                        waited[key] = v
                        engobj.wait_ge(sems[key], v)
                    bi = I["emit"](engobj)
                    s = I["sig"]
                    if s is not None:
                        bi.then_inc(sems[(s[0], s[1])], 16 if I["dma"] else 1)

            @block.tensor
            def _(e):
                run("pe", e)

            @block.scalar
            def _(e):
                run("act", e)

            @block.vector
            def _(e):
                run("dve", e)

            @block.gpsimd
            def _(e):
                run("pool", e)

            @block.sync
            def _(e):
                run("sp", e)
                for k in range(min(self.NDMA, dma_k)):
                    if dma_val[k] > 0:
                        e.wait_ge(sems[("dma", k)], dma_val[k])
        return len(ins)


class KB:
    def __init__(self, nc):
        self.nc = nc
        self.P = Prog(nc)
        self.rr = 0

    def mm(self, out, lhsT, rhs, start=True, stop=True):
        self.P.add("pe", lambda e: e.matmul(out, lhsT, rhs, start=start, stop=stop),
                   reads=[lhsT, rhs], writes=[out])

    def tr(self, out, in_, ident):
        self.P.add("pe", lambda e: e.transpose(out, in_, ident), reads=[in_, ident], writes=[out])

    def act(self, out, in_, func, scale=None, bias=None, accum=None, eng="act"):
        kw = {}
        rd = [in_]
        wr = [out]
        if scale is not None:
            kw["scale"] = scale
            if not isinstance(scale, (int, float)):
                rd.append(scale)
        if bias is not None:
            kw["bias"] = bias
            if not isinstance(bias, (int, float)):
                rd.append(bias)
        if accum is not None:
            kw["accum_out"] = accum
            wr.append(accum)
        self.P.add("act", lambda e: e.activation(out=out, in_=in_, func=func, **kw), reads=rd, writes=wr)

    def tt(self, eng, out, a, b, op):
        self.P.add(eng, lambda e: e.tensor_tensor(out=out, in0=a, in1=b, op=op), reads=[a, b], writes=[out])

    def ts(self, eng, out, a, s1, s2, op0, op1=None):
        rd = [a]
        if not isinstance(s1, (int, float)):
            rd.append(s1)
        if s2 is not None and not isinstance(s2, (int, float)):
            rd.append(s2)
        if op1 is None:
            self.P.add(eng, lambda e: e.tensor_scalar(out=out, in0=a, scalar1=s1, scalar2=None, op0=op0),
                       reads=rd, writes=[out])
        else:
            self.P.add(eng, lambda e: e.tensor_scalar(out=out, in0=a, scalar1=s1, scalar2=s2, op0=op0, op1=op1),
                       reads=rd, writes=[out])

    def stt(self, eng, out, a, scalar, b, op0, op1):
        rd = [a, b]
        if not isinstance(scalar, (int, float)):
            rd.append(scalar)
        self.P.add(eng, lambda e: e.scalar_tensor_tensor(out=out, in0=a, scalar=scalar, in1=b, op0=op0, op1=op1),
                   reads=rd, writes=[out])

    def cp(self, eng, out, in_):
        if eng == "act":
            self.act(out, in_, AF.Copy)
        else:
            self.P.add(eng, lambda e: e.tensor_copy(out=out, in_=in_), reads=[in_], writes=[out])

    def recip(self, out, in_):
        self.P.add("dve", lambda e: e.reciprocal(out=out, in_=in_), reads=[in_], writes=[out])

    def memset(self, eng, out, val):
        self.P.add(eng, lambda e: e.memset(out, val), writes=[out])

    def aselect(self, out, in_, pattern, op, fill, base, cm):
        self.P.add("pool", lambda e: e.affine_select(out=out, in_=in_, pattern=pattern, compare_op=op,
                                                     fill=fill, base=base, channel_multiplier=cm),
                   reads=[in_], writes=[out])

    def dma(self, eng, out, in_):
        self.P.add(eng, lambda e: e.dma_start(out=out, in_=in_), reads=[in_], writes=[out], dma=True)


def build(flags=None, dbg=None):
    flags = flags or dict(A=True, B=True, C=True, FFN=True)
    nc = bass.Bass("TRN2", target_bir_lowering=False)
    K = KB(nc)

    def din(name, shape, dt=F32):
        return nc.dram_tensor(name, list(shape), dt, kind="ExternalInput").ap()

    x_d = din("x", [T, D])
    pos_d = din("pos", [1, T], I32)
    norm_mix_d = din("norm_mix", [DEPTH, D])
    w_in_d = din("w_in", [DEPTH, D, INW])
    w_o_d = din("w_o", [DEPTH, D, D])
    norm_ffn_d = din("norm_ffn", [DEPTH, D])
    w_up_d = din("w_up", [DEPTH, NFT, 128, 8 * 256])
    w_down_d = din("w_down", [DEPTH, DFF, D])
    norm_final_d = din("norm_final", [1, D])
    pp_d = din("pp", [DEPTH, 128, NPP])
    bc_d = din("bc", [DEPTH, 128, 16])
    cst_d = din("cst", [128, 512])
    out_d = nc.dram_tensor("out", [T, D], F32, kind="ExternalOutput").ap()
    dbg_out = {}

    def dbg_store(name, ap, eng="sp"):
        if dbg is None or name not in dbg:
            return
        o = nc.dram_tensor("dbg_" + name, list(ap.shape), ap.dtype, kind="ExternalOutput").ap()
        dbg_out[name] = o
        K.dma(eng, o, ap)

    XT = nc.alloc_sbuf_tensor("XT", [128, NT, D], F32)
    HT = nc.alloc_sbuf_tensor("HT", [128, 8, T], BF16)
    SCR_B = 90 * 1024
    SCR = nc.alloc_sbuf_tensor("SCR", [128, SCR_B // 2], BF16)
    PPT = nc.alloc_sbuf_tensor("PPT", [128, NPP], F32)
    BCT = nc.alloc_sbuf_tensor("BCT", [128, 16], F32)
    CST = nc.alloc_sbuf_tensor("CST", [128, 512], F32)
    GBC = nc.alloc_sbuf_tensor("GBC", [128, D], F32)
    IDB = nc.alloc_sbuf_tensor("IDB", [128, 128], BF16)
    IDF = nc.alloc_sbuf_tensor("IDF", [128, 128], F32)
    ONF = nc.alloc_sbuf_tensor("ONF", [128, 128], F32)
    ONB = nc.alloc_sbuf_tensor("ONB", [128, 128], BF16)
    STAT = nc.alloc_sbuf_tensor("STAT", [128, 64], F32)
    EPSC = nc.alloc_sbuf_tensor("EPSC", [128, 2], F32)
    BLK64 = nc.alloc_sbuf_tensor("BLK64", [128, 128], F32)
    SEL = nc.alloc_sbuf_tensor("SEL", [65, 64], F32)
    IDF2 = nc.alloc_sbuf_tensor("IDF2", [128, 2, 128], F32)
    AMASK = nc.alloc_sbuf_tensor("AMASK", [128, 2, 128], BF16)
    COS = nc.alloc_sbuf_tensor("COS", [128, T], BF16)
    TAB = nc.alloc_sbuf_tensor("TAB", [128, 6, 128], F32)
    SINS = nc.alloc_sbuf_tensor("SINS", [128, T], BF16)
    PS = [nc.alloc_psum_tensor(f"PS{i}", [128, 1024], F32) for i in range(3)]
    PT = nc.alloc_psum_tensor("PT", [128, 1024], BF16)
    PH = nc.alloc_psum_tensor("PH", [128, 512], F32)
    PH_ = PH
    print("sbuf bytes remaining", nc.sbuf_bytes_remaining)

    def carve(off, shape, dt):
        isz = mybir.dt.size(dt)
        n = int(np.prod(shape[1:]))
        assert off % 4 == 0 and off + n * isz <= SCR_B, (off, shape)
        v = SCR[:, off // 2: off // 2 + n * isz // 2]
        if dt != BF16:
            v = v.bitcast(dt)
        if len(shape) == 3:
            v = v.rearrange("p (a b) -> p a b", a=shape[1])
        elif len(shape) == 4:
            v = v.rearrange("p (a b c) -> p a b c", a=shape[1], b=shape[2])
        return v

    WBO = 0
    WB = [carve(WBO + i * 4096, [128, 8, 256], BF16) for i in range(3)]
    FBO = 12288
    FB = [carve(FBO + i * 8192, [128, T], F32) for i in range(3)]
    RO = FBO + 3 * 8192

    psn = [0]

    def next_ps():
        p = PS[psn[0] % 3]
        psn[0] += 1
        return p

    wbn = [0]

    def next_wb():
        w = WB[wbn[0] % 3]
        wbn[0] += 1
        return w

    K.dma("sp", CST[:, :], cst_d[:, :])
    K.memset("pool", IDF[:, :], 1.0)
    K.aselect(IDF[:, :], IDF[:, :], [[-1, 128]], ALU.is_equal, 0.0, 0, 1)
    K.cp("pool", IDB[:, :], IDF[:, :])
    K.memset("pool", ONF[:, :], 1.0)
    K.memset("pool", ONB[:, :], 1.0)
    K.memset("pool", EPSC[:, 0:1], EPS)
    K.memset("pool", EPSC[:, 1:2], -float(np.pi))
    K.memset("pool", BLK64[:, :], 0.0)
    K.memset("pool", BLK64[0:64, 0:64], 1.0)
    K.memset("pool", BLK64[64:128, 64:128], 1.0)
    K.memset("pool", SEL[:, :], 0.0)
    K.memset("pool", SEL[64:65, :], 1.0)
    K.memset("pool", IDF2[:, :, :], 1.0)
    K.aselect(IDF2[:, 0, :], IDF2[:, 0, :], [[-1, 128]], ALU.is_ge, 0.0, 0, 1)
    K.aselect(IDF2[:, 1, :], IDF2[:, 1, :], [[1, 128]], ALU.is_ge, 0.0, 0, -1)
    K.cp("pool", AMASK[:, :, :], IDF2[:, :, :])
    posi = carve(FBO, [128, T], I32)
    K.dma("sp", posi, pos_d[0, :].partition_broadcast(128))
    posf = FB[1]
    K.cp("dve", posf[:, :], posi)
    TWO_PI = 2.0 * float(np.pi)

    def sin_table(dst, phase, scale_ap):
        x = FB[2]
        if phase == 0.0:
            K.ts("dve", x[:, :], posf[:, :], CST[:, 0:1], None, ALU.mult)
        else:
            K.ts("dve", x[:, :], posf[:, :], CST[:, 0:1], phase, ALU.mult, ALU.add)
        kf = FB[0]
        K.ts("dve", kf[:, :], x[:, :], 1.0 / TWO_PI, None, ALU.mult)
        ki = carve(FBO, [128, T], I32)
        K.cp("dve", ki, kf[:, :])
        K.cp("dve", kf[:, :], ki)
        K.stt("dve", x[:, :], kf[:, :], -TWO_PI, x[:, :], ALU.mult, ALU.add)
        K.ts("dve", kf[:, :], x[:, :], float(np.pi), TWO_PI, ALU.is_gt, ALU.mult)
        K.tt("dve", x[:, :], x[:, :], kf[:, :], ALU.subtract)
        K.act(kf[:, :], x[:, :], AF.Sin)
        if scale_ap is None:
            K.cp("dve", dst, kf[:, :])
        else:
            K.ts("dve", dst, kf[:, :], scale_ap, None, ALU.mult)

    sin_table(SINS[:, :], 0.0, CST[:, 1:2])
    sin_table(COS[:, :], 0.5 * float(np.pi), None)
    dbg_store("cos", COS[:, :])
    dbg_store("sins", SINS[:, :])

    xv = x_d.rearrange("(n p) d -> p n d", p=128)
    for i in range(NT):
        K.dma("sp" if i % 2 == 0 else "act", XT[:, i, :], xv[:, i, :])

    elt = [0]

    def rr2():
        elt[0] += 1
        return "dve" if elt[0] % 2 else "pool"

    def rmsnorm_to_HT(g_row):
        K.dma("sp", GBC[:, :], g_row.partition_broadcast(128))
        junk = carve(FBO, [128, D], BF16)
        hn = [carve(FBO + 8192, [128, D], BF16), carve(FBO + 8192 + 2048, [128, D], BF16)]
        junk2 = carve(FBO + 2048, [128, D], BF16)
        for i in range(NT):
            if i % 2 == 0:
                K.act(junk, XT[:, i, :], AF.Square, accum=STAT[:, i:i + 1])
            else:
                K.P.add("dve", lambda e, i=i: e.scalar_tensor_tensor(out=junk2, in0=XT[:, i, :], scalar=1.0, in1=XT[:, i, :],
                                                                      op0=ALU.mult, op1=ALU.mult, accum_out=STAT[:, i:i + 1]),
                        reads=[XT[:, i, :]], writes=[junk2, STAT[:, i:i + 1]])
        K.ts("dve", STAT[:, 16:32], STAT[:, 0:16], 1.0 / D, EPS, ALU.mult, ALU.add)
        K.act(STAT[:, 16:32], STAT[:, 16:32], AF.Sqrt)
        K.recip(STAT[:, 32:48], STAT[:, 16:32])
        for i in range(NT):
            h = hn[i % 2]
            K.stt("dve", h, XT[:, i, :], STAT[:, 32 + i:33 + i], GBC[:, :], ALU.mult, ALU.mult)
            pbank = PT[:, :] if i % 2 == 0 else PH_[:, :].bitcast(BF16)
            ptv = pbank.rearrange("p (c t) -> p c t", c=8)
            for c in range(8):
                K.tr(ptv[:, c, :], h[:, c * 128:(c + 1) * 128], IDB[:, :])
            K.cp("act", HT[:, :, i * 128:(i + 1) * 128], ptv)

    def load_w(dst, w2d, c0, n, k0=0, kn=8):
        src = w2d[k0 * 128:(k0 + kn) * 128, c0:c0 + n].rearrange("(k p) c -> p k c", p=128)
        K.dma("pool", dst, src)

    def projT(wt, ps, t0, tn):
        for b0 in range(0, tn, 512):
            bn = min(512, tn - b0)
            for kc in range(8):
                K.mm(ps[:, b0:b0 + bn], wt[:, kc, :], HT[:, kc, t0 + b0:t0 + b0 + bn], start=(kc == 0), stop=(kc == 7))

    def ffn(l):
        PHS = [PH_, PT[:, :].bitcast(F32)]
        rmsnorm_to_HT(norm_ffn_d[l, :])
        TB = 512
        stg = [carve(FBO + i * 2048, [128, TB], F32) for i in range(6)]
        actT = carve(FBO + 12288, [128, NFT, TB], BF16)
        WD = carve(FBO + 12288 + 22528, [128, NFT, D], BF16)
        wd2 = w_down_d[l]
        for ft in range(NFT):
            K.dma("pool", WD[:, ft, :], wd2[ft * 128:(ft + 1) * 128, :])
        wu2 = w_up_d[l]
        sn = 0
        ntb = flags.get('tbl', T // TB)
        nft = flags.get('ftl', NFT)
        units = [(tb, ft) for tb in range(ntb) for ft in range(nft)]
        wbs = {}

        def issue(u):
            wbs[u] = WB[u % 3]
            K.dma("pool", wbs[u][:, :, :], wu2[units[u][1]].rearrange("p (k c) -> p k c", k=8))

        issue(0)
        if len(units) > 1:
            issue(1)
        for u, (tb, ft) in enumerate(units):
            t0 = tb * TB
            if True:
                if u + 2 < len(units):
                    issue(u + 2)
                wb = wbs[u]
                ps = next_ps()
                cs = []
                for half in range(2):
                    wt = wb[:, :, half * 128:(half + 1) * 128]
                    pv = ps[:, half * 512:(half + 1) * 512]
                    projT(wt, pv, t0, TB)
                    hcol = (ft % 8) * 4 + half * 2
                    PH = PHS[u % 2]
                    if 0 < tb < T // TB - 1:
                        for kc in range(8):
                            K.mm(PH[:, hcol:hcol + 2], wt[:, kc, :], HT[:, kc, t0 - 1:t0 + TB + 1:TB + 1], start=(kc == 0), stop=(kc == 7))
                    elif tb > 0:
                        for kc in range(8):
                            K.mm(PH[:, hcol:hcol + 1], wt[:, kc, :], HT[:, kc, t0 - 1:t0], start=(kc == 0), stop=(kc == 7))
                    else:
                        for kc in range(8):
                            K.mm(PH[:, hcol + 1:hcol + 2], wt[:, kc, :], HT[:, kc, t0 + TB:t0 + TB + 1], start=(kc == 0), stop=(kc == 7))
                    col = 60 + (half * NFT + ft) * 3
                    w0, w1, w2 = PPT[:, col:col + 1], PPT[:, col + 1:col + 2], PPT[:, col + 2:col + 3]
                    c = stg[sn % 6]
                    sn += 1
                    K.act(c, pv, AF.Copy, scale=w1)
                    K.stt("dve", c[:, 1:TB], pv[:, 0:TB - 1], w0, c[:, 1:TB], ALU.mult, ALU.add)
                    K.stt("dve", c[:, 0:TB - 1], pv[:, 1:TB], w2, c[:, 0:TB - 1], ALU.mult, ALU.add)
                    if tb > 0:
                        K.stt("dve", c[:, 0:1], PH[:, hcol:hcol + 1], w0, c[:, 0:1], ALU.mult, ALU.add)
                    if tb < T // TB - 1:
                        K.stt("dve", c[:, TB - 1:TB], PH[:, hcol + 1:hcol + 2], w2, c[:, TB - 1:TB], ALU.mult, ALU.add)
                    cs.append(c)
                K.act(cs[0], cs[0], AF.Silu)
                K.tt("pool", actT[:, ft, :], cs[0], cs[1], ALU.mult)
            if ft != nft - 1:
                continue
            for ti in range(TB // 128):
                tile_i = tb * (TB // 128) + ti
                ps = next_ps()
                for ch in range(2):
                    nf = flags.get('ftl', NFT)
                    for ft in range(nf):
                        K.mm(ps[:, ch * 512:(ch + 1) * 512], actT[:, ft, ti * 128:(ti + 1) * 128],
                             WD[:, ft, ch * 512:(ch + 1) * 512], start=(ft == 0), stop=(ft == nf - 1))
                K.tt("dve", XT[:, tile_i, :], XT[:, tile_i, :], ps[:, :], ALU.add)


    def wo_pass(l, lhs_fn, nchunks, kparts, row0):
        WO = carve(WBO, [128, nchunks, D], BF16)
        for c in range(nchunks):
            K.dma("pool", WO[0:kparts, c, :], w_o_d[l][row0 + c * kparts:row0 + (c + 1) * kparts, :])
        for i in range(NT):
            ps = next_ps()
            for ch in range(2):
                for c in range(nchunks):
                    K.mm(ps[:, ch * 512:(ch + 1) * 512], lhs_fn(c, i), WO[0:kparts, c, ch * 512:(ch + 1) * 512],
                         start=(c == 0), stop=(c == nchunks - 1))
            K.tt("dve", XT[:, i, :], XT[:, i, :], ps[:, :], ALU.add)

    def mixer_A(l, Y):
        w2 = w_in_d[l]
        for ct in range(2):
            wts = {}
            for name, c0 in (("xa", C_XA), ("gc", C_GC), ("gb", C_GB)):
                wb = next_wb()
                load_w(wb[:, :, 0:128], w2, c0 + ct * 128, 128)
                wts[name] = wb[:, :, 0:128]
            for half in range(2):
                hs = slice(half * 1024, (half + 1) * 1024)
                ps = next_ps()
                projT(wts["xa"], ps, half * 1024, 1024)
                K.cp("act", FB[0][:, hs], ps[:, :])
            for half in range(2):
                hs = slice(half * 1024, (half + 1) * 1024)
                ps = next_ps()
                projT(wts["gc"], ps, half * 1024, 1024)
                K.tt("dve", FB[1][:, hs], ps[:, :], FB[0][:, hs], ALU.mult)
            w0, w1, w2c = (PPT[:, ct * 3 + k:ct * 3 + k + 1] for k in range(3))
            K.ts("dve", FB[2][:, :], FB[1][:, :], w1, None, ALU.mult)
            K.stt("dve", FB[2][:, 1:T], FB[1][:, 0:T - 1], w0, FB[2][:, 1:T], ALU.mult, ALU.add)
            K.stt("dve", FB[2][:, 0:T - 1], FB[1][:, 1:T], w2c, FB[2][:, 0:T - 1], ALU.mult, ALU.add)
            for half in range(2):
                hs = slice(half * 1024, (half + 1) * 1024)
                ps = next_ps()
                projT(wts["gb"], ps, half * 1024, 1024)
                K.tt("dve", FB[0][:, hs], ps[:, :], FB[2][:, hs], ALU.mult)
            K.act(FB[1][:, :], FB[0][:, :], AF.Square)
            for half in range(2):
                hs = slice(half * 1024, (half + 1) * 1024)
                ps = next_ps()
                for b in range(2):
                    K.mm(ps[:, b * 512:(b + 1) * 512], BLK64[:, :], FB[1][:, half * 1024 + b * 512:half * 1024 + (b + 1) * 512])
                K.act(FB[2][:, hs], ps[:, :], AF.Ln, scale=1.0 / 64, bias=EPSC[:, 0:1])
            K.act(FB[2][:, :], FB[2][:, :], AF.Exp, scale=-0.5)
            K.stt("dve", Y[:, ct, :], FB[0][:, :], PPT[:, 6 + ct:7 + ct], FB[2][:, :], ALU.mult, ALU.mult)

    HBO = RO + 24576

    def bc_last(ap, n):
        return bass.AP(ap.tensor, ap.offset, [list(d) for d in ap.ap] + [[0, n]])

    def bc_mid(ap, n):
        return bass.AP(ap.tensor, ap.offset, [list(ap.ap[0]), [0, n]] + [list(d) for d in ap.ap[1:]])

    def mixer_B(l, Y):
        w2 = w_in_d[l]
        QTn = carve(HBO, [128, T], BF16)
        KTn = carve(HBO + 4096, [128, T], BF16)
        VT = carve(HBO + 8192, [128, T], BF16)
        KD = carve(HBO + 12288, [128, NT, 128], BF16)
        VB = carve(HBO + 16384, [128, NT, 128], BF16)
        QG = carve(HBO + 20480, [128, T], BF16)
        tmpq = [carve(HBO + 24576 + i * 1024, [128, 4, 128], BF16) for i in range(6)]
        F2 = FBO + 16384
        ND = carve(F2, [128, 4, 128], F32)
        DM = carve(F2 + 2048, [128, 4, 128], F32)
        QKT = carve(F2 + 4096, [128, 4, 128], BF16)
        RR = carve(F2 + 5120, [128, 128], BF16)
        VN = carve(F2 + 5376, [128, 128], BF16)
        SBF = carve(F2 + 5632, [128, 128], BF16)
        S32 = carve(F2 + 5888, [128, 128], F32)
        STRICT = carve(F2 + 6400, [128, 2, 128], BF16)
        A0 = carve(F2 + 6912, [128, 4, 128], BF16)
        F1 = FBO + 8192
        UUS = [carve(F1 + i * 1024, [128, 4, 128], BF16) for i in range(2)]
        PHIS = [carve(F1 + 2048 + i * 1024, [128, 4, 128], BF16) for i in range(2)]
        PSIS = [carve(F1 + 4096 + i * 1024, [128, 4, 128], BF16) for i in range(2)]
        QKTS = [carve(F1 + 6144 + i * 1024, [128, 4, 128], BF16) for i in range(2)]
        SSV = [carve(F2 + 4352 + i * 256, [128, 128], BF16) for i in range(4)] + \
              [carve(RO + 7168 + i * 256, [128, 128], BF16) for i in range(4)]
        PTF = PT[:, :].bitcast(F32)
        TQS = [tmpq + [A0], [carve(RO + i * 1024, [128, 4, 128], BF16) for i in range(7)]]
        OACC = FB[0].rearrange("p (i d) -> p i d", i=NT)
        BETA, GC, EGC, EKD, EG, NBG = (TAB[:, k, :] for k in range(6))

        def col(d, h, i):
            return d * 64 + h * 16 + i

        K.cp("pool", STRICT[:, 0, :], IDF2[:, 0, :])
        K.aselect(STRICT[:, 0, :], STRICT[:, 0, :], [[-1, 128]], ALU.not_equal, 0.0, 0, 1)
        K.cp("pool", STRICT[:, 1, :], IDF2[:, 1, :])
        K.aselect(STRICT[:, 1, :], STRICT[:, 1, :], [[-1, 128]], ALU.not_equal, 0.0, 0, 1)

        wb = next_wb()
        load_w(wb[:, :, 0:16], w2, C_SC, 16)
        for i in range(NT):
            for kc in range(8):
                K.mm(PH[:, i * 16:(i + 1) * 16], HT[:, kc, i * 128:(i + 1) * 128], wb[:, kc, 0:16],
                     start=(kc == 0), stop=(kc == 7))
        SC = FB[1][:, 0:256].rearrange("p (i c) -> p i c", i=NT)
        K.cp("act", SC, PH[:, 0:256].rearrange("p (i c) -> p i c", i=NT))
        T8 = lambda ap: ap.rearrange("p (c i) -> p c i", c=8)
        K.act(T8(BETA), SC[:, :, 0:8].rearrange("p i c -> p c i"), AF.Sigmoid)
        XA = FB[1][:, 256:384]
        K.tt("dve", T8(XA), SC[:, :, 8:16].rearrange("p i c -> p c i"), bc_last(BCT[:, 8:16], NT), ALU.add)
        K.act(XA, XA, AF.Exp)
        K.act(XA, XA, AF.Ln, bias=ONF[:, 0:1])
        EA = FB[1][:, 384:392]
        K.act(EA, BCT[:, 0:8], AF.Exp)
        G = FB[1][:, 512:640]
        K.stt("dve", T8(G), T8(XA), -1.0, bc_last(EA, NT), ALU.mult, ALU.mult)
        K.mm(PH[:, 0:64], IDF2[:, 1, :], G[:, 0:64])
        K.mm(PH[:, 64:128], IDF2[:, 0, :], G[:, 64:128])
        K.mm(PH[:, 128:256], ONF[:, :], G[:, 0:128])
        K.cp("dve", GC, PH[:, 0:128])
        K.act(EGC, PH[:, 0:128], AF.Exp)
        K.act(EG, PH[:, 128:256], AF.Exp)
        K.tt("dve", EKD, PH[:, 128:256], GC, ALU.subtract)
        K.act(EKD, EKD, AF.Exp)
        K.stt("dve", NBG, BETA, -1.0, EGC, ALU.mult, ALU.mult)
        GCH = carve(F2 + 5376, [128, 128], BF16)
        GCL = carve(F2 + 5632, [128, 128], BF16)
        K.cp("act", GCH, GC)
        GTMP = FB[1][:, 1024:1152]
        K.tt("dve", GTMP, GC, GCH, ALU.subtract)
        K.cp("act", GCL, GTMP)
        EGCB = carve(F2 + 7936, [128, 128], BF16)
        K.cp("act", EGCB, EGC)
        if l == 0:
            dbg_store("tab", TAB[:, :, :])

        for h in range(4):
            X3 = carve(HBO + 12288, [128, T], F32)
            EDGE = carve(F2 + 4096, [128, 8], F32)

            def prep_chain(wi, c0w, dst, cb):
                Pp = PS[wi]
                wb = WB[wi]
                load_w(wb[:, :, 0:128], w2, c0w + h * 128, 128)
                pc = 8 + (wi * 4 + h) * 3
                w0, w1, w2c = (PPT[:, pc + k:pc + k + 1] for k in range(3))
                for half in range(2):
                    o = half * 1024
                    projT(wb[:, :, 0:128], Pp, o, 1024)
                    K.act(cb[:, o:o + 1024], Pp[:, :], AF.Copy, scale=w1)
                    yield
                    K.stt("dve", cb[:, o + 1:o + 1024], Pp[:, 0:1023], w0, cb[:, o + 1:o + 1024], ALU.mult, ALU.add)
                    yield
                    K.stt("dve", cb[:, o:o + 1023], Pp[:, 1:1024], w2c, cb[:, o:o + 1023], ALU.mult, ALU.add)
                    if half == 0:
                        K.cp("act", EDGE[:, wi * 2:wi * 2 + 1], Pp[:, 1023:1024])
                    else:
                        K.cp("act", EDGE[:, wi * 2 + 1:wi * 2 + 2], Pp[:, 0:1])
                    yield
                K.stt("dve", cb[:, 1023:1024], EDGE[:, wi * 2 + 1:wi * 2 + 2], w2c, cb[:, 1023:1024], ALU.mult, ALU.add)
                K.stt("dve", cb[:, 1024:1025], EDGE[:, wi * 2:wi * 2 + 1], w0, cb[:, 1024:1025], ALU.mult, ALU.add)
                K.act(cb[:, :], cb[:, :], AF.Silu)
                yield
                if wi == 2:
                    K.cp("dve", dst[:, :], cb[:, :])
                    return
                K.act(dst[:, :], cb[:, :], AF.Square)
                yield
                for half in range(2):
                    o = half * 1024
                    for b2 in range(2):
                        K.mm(Pp[:, b2 * 512:(b2 + 1) * 512], ONB[:, :], dst[:, o + b2 * 512:o + (b2 + 1) * 512])
                    K.act(Pp[:, :], Pp[:, :], AF.Ln, bias=EPSC[:, 0:1])
                    yield
                    K.act(dst[:, o:o + 1024], Pp[:, :], AF.Exp, scale=-0.5)
                    yield
                K.stt("dve", dst[:, :], cb[:, :], (128.0 ** -0.5) if wi == 0 else 1.0, dst[:, :], ALU.mult, ALU.mult)
                yield

            chains = [prep_chain(0, C_QD, QTn, FB[0]), prep_chain(1, C_KD, KTn, FB[1]), prep_chain(2, C_VD, VT, X3)]
            while chains:
                for g in list(chains):
                    try:
                        next(g)
                    except StopIteration:
                        chains.remove(g)
            gwb = WB[0]
            load_w(gwb[:, :, 0:128], w2, C_GD + h * 128, 128)
            if l == 0 and h == 0:
                dbg_store("qtn", QTn[:, :])
                dbg_store("ktn", KTn[:, :])
                dbg_store("vt", VT[:, :])

            for d in range(2):
                c0 = col(d, h, 0)
                for src, dstt, tab in ((KTn, KD, EKD), (VT, VB, BETA)):
                    for g8 in range(2):
                        ps = next_ps()
                        for q in range(8):
                            i = g8 * 8 + q
                            K.mm(ps[:, q * 128:(q + 1) * 128], src[:, i * 128:(i + 1) * 128], IDB[:, :])
                        K.tt("dve", dstt[:, g8 * 8:(g8 + 1) * 8, :], ps[:, :].rearrange("p (q d) -> p q d", q=8),
                             bc_last(tab[:, c0 + g8 * 8:c0 + (g8 + 1) * 8], 128), ALU.mult)
                for g8 in range(2):
                    ps = next_ps()
                    for q in range(8):
                        i = g8 * 8 + q
                        K.mm(ps[:, q * 128:(q + 1) * 128],
                             bass.AP(EGCB.tensor, EGCB[:, c0 + i:c0 + i + 1].offset, [list(EGCB.ap[0]), [0, 128]]), IDB[:, :])
                    K.tt("dve", QG[:, g8 * 1024:(g8 + 1) * 1024], ps[:, :], QTn[:, g8 * 1024:(g8 + 1) * 1024], ALU.mult)
                K.memset("pool", S32[:, :], 0.0)
                K.memset("pool", SSV[0][:, :], 0.0)
                quads = list(range(4)) if d == 0 else list(range(3, -1, -1))
                ipn = [0]

                def inv_ps():
                    ipn[0] += 1
                    return PS[ipn[0] % 2]

                def inv_gen(qd, par, tset):
                    i0 = qd * 4
                    QKTo = QKTS[par]
                    tmpq = TQS[tset][0:6]
                    A0 = TQS[tset][6]
                    Dm = (ND, DM)[tset]
                    PQ = PS[tset]
                    MB = PS[2][:, tset * 512:(tset + 1) * 512]
                    psr = MB
                    for q in range(4):
                        cc = c0 + i0 + q
                        K.mm(psr[:, q * 128:(q + 1) * 128],
                             bass.AP(GCH.tensor, GCH[:, cc:cc + 1].offset, [list(GCH.ap[0]), [0, 128]]), IDB[:, :],
                             start=True, stop=False)
                        K.mm(psr[:, q * 128:(q + 1) * 128],
                             bass.AP(GCL.tensor, GCL[:, cc:cc + 1].offset, [list(GCL.ap[0]), [0, 128]]), IDB[:, :],
                             start=False, stop=True)
                    psk = PQ
                    for q in range(4):
                        ts_ = slice((i0 + q) * 128, (i0 + q + 1) * 128)
                        K.mm(psk[:, q * 128:(q + 1) * 128], KTn[:, ts_], KTn[:, ts_])
                    for q in range(4):
                        ts_ = slice((i0 + q) * 128, (i0 + q + 1) * 128)
                        K.mm(psk[:, 512 + q * 128:512 + (q + 1) * 128], QTn[:, ts_], KTn[:, ts_])
                    K.tt("dve", Dm, psr[:, 0:512].rearrange("p (q s) -> p q s", q=4),
                         bc_last(GC[:, c0 + i0:c0 + i0 + 4], 128), ALU.subtract)
                    yield
                    K.act(Dm, Dm, AF.Exp, scale=-1.0)
                    yield
                    K.stt("dve", Dm, Dm, 1.0, bc_mid(IDF2[:, d, :], 4), ALU.min, ALU.mult)
                    TQ = tmpq[5]
                    K.tt("dve", TQ, psk[:, 512:1024].rearrange("p (q s) -> p q s", q=4), Dm, ALU.mult)
                    yield
                    K.tt("pool", Dm, Dm, bc_mid(STRICT[:, d, :], 4), ALU.mult)
                    K.tt("pool", Dm, Dm, bc_last(BETA[:, c0 + i0:c0 + i0 + 4], 128), ALU.mult)
                    yield
                    K.tt("dve", A0, psk[:, 0:512].rearrange("p (q s) -> p q s", q=4), Dm, ALU.mult)
                    yield
                    ptv = PT[:, :].rearrange("p (a q s) -> p a q s", a=2, q=4)
                    for q in range(4):
                        K.tr(ptv[:, 0, q, :], A0[:, q, :], IDB[:, :])
                    for q in range(4):
                        K.tr(ptv[:, 1, q, :], TQ[:, q, :], IDB[:, :])
                    P0 = tmpq[1]
                    K.cp("act", P0, ptv[:, 0, :, :])
                    K.cp("act", QKTo, ptv[:, 1, :, :])
                    M0 = tmpq[2]
                    K.tt("pool", M0, bc_mid(IDB[:, :], 4), P0, ALU.subtract)
                    yield
                    Pc, Qc, Mc = P0, A0, M0
                    free = [tmpq[3], tmpq[4], tmpq[5], tmpq[0]]
                    for lev in range(1, 6):
                        psp = PQ
                        last = (lev == 5)
                        for q in range(4):
                            K.mm(psp[:, 512 + q * 128:512 + (q + 1) * 128], Pc[:, q, :], Qc[:, q, :])
                        if not last:
                            for q in range(4):
                                K.mm(psp[:, q * 128:(q + 1) * 128], Qc[:, q, :], Pc[:, q, :])
                        Qn = free.pop(0)
                        K.cp("act", Qn, psp[:, 512:1024].rearrange("p (q s) -> p q s", q=4))
                        if not last:
                            Pn = free.pop(0)
                            K.cp("act", Pn, psp[:, 0:512].rearrange("p (q s) -> p q s", q=4))
                        yield
                        psm = MB
                        for q in range(4):
                            K.mm(psm[:, q * 128:(q + 1) * 128], Qn[:, q, :], Mc[:, q, :])
                        Mn = free.pop(0)
                        K.tt("dve", Mn, psm[:, 0:512].rearrange("p (q s) -> p q s", q=4), Mc, ALU.add)
                        free.append(Mc)
                        if Qc is not A0:
                            free.append(Qc)
                        if not last:
                            free.append(Pc)
                            Pc = Pn
                        Qc, Mc = Qn, Mn
                        yield
                    avail = [t for t in TQS[tset] if t is not Mc and t is not A0]
                    XTb, Eb, MLO, KBG, WN = avail
                    for q in range(4):
                        K.tr(ptv[:, 0, q, :], Mc[:, q, :], IDB[:, :])
                    K.cp("act", XTb, ptv[:, 0, :, :])
                    pse = MB
                    for q in range(4):
                        K.mm(pse[:, q * 128:(q + 1) * 128], A0[:, q, :], Mc[:, q, :])
                    K.tt("pool", Eb, bc_mid(IDB[:, :], 4), Mc, ALU.subtract)
                    K.tt("dve", Eb, Eb, pse[:, 0:512].rearrange("p (q s) -> p q s", q=4), ALU.subtract)
                    yield
                    psl = PQ
                    for q in range(4):
                        K.mm(psl[:, q * 128:(q + 1) * 128], XTb[:, q, :], Eb[:, q, :])
                    K.cp("act", MLO, psl[:, 0:512].rearrange("p (q s) -> p q s", q=4))
                    yield
                    UU, PHI, PSI = UUS[par], PHIS[par], PSIS[par]
                    for q in range(4):
                        ts_ = slice((i0 + q) * 128, (i0 + q + 1) * 128)
                        K.mm(PQ[:, q * 128:(q + 1) * 128], KTn[:, ts_], IDB[:, :])
                    K.tt("dve", KBG, PQ[:, 0:512].rearrange("p (q s) -> p q s", q=4),
                         bc_last(NBG[:, c0 + i0:c0 + i0 + 4], 128), ALU.mult)
                    yield
                    for q in range(4):
                        K.mm(PQ[:, 512 + q * 128:512 + (q + 1) * 128], Mc[:, q, :], KBG[:, q, :], start=True, stop=False)
                        K.mm(PQ[:, 512 + q * 128:512 + (q + 1) * 128], MLO[:, q, :], KBG[:, q, :], start=False, stop=True)
                    for q in range(4):
                        K.mm(MB[:, q * 128:(q + 1) * 128], Mc[:, q, :], VB[:, i0 + q, :], start=True, stop=False)
                        K.mm(MB[:, q * 128:(q + 1) * 128], MLO[:, q, :], VB[:, i0 + q, :], start=False, stop=True)
                    K.cp("act", WN, PQ[:, 512:1024].rearrange("p (q s) -> p q s", q=4))
                    K.cp("dve", UU, MB.rearrange("p (q s) -> p q s", q=4))
                    yield
                    for q in range(4):
                        K.mm(PQ[:, q * 128:(q + 1) * 128], WN[:, q, :], KD[:, i0 + q, :])
                    for q in range(4):
                        K.mm(PQ[:, 512 + q * 128:512 + (q + 1) * 128], WN[:, q, :], QKTo[:, q, :])
                    K.cp("act", PHI, PQ[:, 0:512].rearrange("p (q s) -> p q s", q=4))
                    K.tt("dve", PSI, PQ[:, 512:1024].rearrange("p (q s) -> p q s", q=4),
                         QG[:, i0 * 128:(i0 + 4) * 128].rearrange("p (q s) -> p q s", q=4), ALU.add)
                    yield

                def scan_gen(qd, par, nbase):
                    i0 = qd * 4
                    UU, PHI = UUS[par], PHIS[par]
                    order = range(4) if d == 0 else range(3, -1, -1)
                    for j, q in enumerate(order):
                        i = i0 + q
                        n_ = nbase + j
                        cc = c0 + i
                        K.mm(PH[:, 0:128], KD[:, i, :], UU[:, q, :], start=True, stop=False)
                        K.mm(PH[:, 0:128], PHI[:, q, :], SSV[n_ % 8][:, :], start=False, stop=True)
                        K.stt("dve", S32[:, :], S32[:, :], EG[:, cc:cc + 1], PH[:, 0:128], ALU.mult, ALU.add)
                        K.cp("act", SSV[(n_ + 1) % 8][:, :], S32[:, :])
                        yield

                def obatch_gen(qd, par, nbase):
                    i0 = qd * 4
                    UU, PSI, QKTo = UUS[par], PSIS[par], QKTS[par]
                    order = list(range(4)) if d == 0 else list(range(3, -1, -1))
                    for j, q in enumerate(order):
                        n_ = nbase + j
                        K.mm(PTF[:, q * 128:(q + 1) * 128], PSI[:, q, :], SSV[n_ % 8][:, :], start=True, stop=False)
                        K.mm(PTF[:, q * 128:(q + 1) * 128], QKTo[:, q, :], UU[:, q, :], start=False, stop=True)
                    ov4 = PTF[:, 0:512].rearrange("p (q s) -> p q s", q=4)
                    if d == 0:
                        K.cp("act", OACC[:, i0:i0 + 4, :], ov4)
                    else:
                        K.tt("dve", OACC[:, i0:i0 + 4, :], OACC[:, i0:i0 + 4, :], ov4, ALU.add)
                    yield

                def drain(g):
                    for _ in g:
                        pass

                def interleave(g1, g2):
                    a1 = a2 = True
                    while a1 or a2:
                        if a1:
                            try:
                                next(g1)
                            except StopIteration:
                                a1 = False
                        if a2:
                            try:
                                next(g2)
                            except StopIteration:
                                a2 = False

                invs = [inv_gen(quads[n], n % 2, n % 2) for n in range(4)]
                active = [invs[0], invs[1]]

                def step_all():
                    for g in list(active):
                        try:
                            next(g)
                        except StopIteration:
                            active.remove(g)

                while invs[0] in active:
                    step_all()
                for n, qd in enumerate(quads):
                    sc = scan_gen(qd, n % 2, 4 * n)
                    active.append(sc)
                    while sc in active:
                        step_all()
                    ob = obatch_gen(qd, n % 2, 4 * n)
                    active.append(ob)
                    while ob in active:
                        step_all()
                    if n + 2 < 4:
                        active.append(invs[n + 2])
                    while n + 1 < 4 and invs[n + 1] in active:
                        step_all()
            if l == 0:
                dbg_store(f"oacc{h}", FB[0][:, :])
            gps = [PS[0], PS[1]]
            for half in range(2):
                projT(gwb[:, :, 0:128], gps[half], half * 1024, 1024)
            SQ = FB[1].rearrange("p (i d) -> p i d", i=NT)
            K.act(SQ, OACC, AF.Square)
            K.P.add("dve", lambda e: e.tensor_reduce(out=STAT[:, 48:64], in_=SQ, axis=AX.X, op=ALU.add),
                    reads=[SQ], writes=[STAT[:, 48:64]])
            K.ts("dve", STAT[:, 48:64], STAT[:, 48:64], 1.0 / 128, EPS, ALU.mult, ALU.add)
            K.act(STAT[:, 48:64], STAT[:, 48:64], AF.Sqrt)
            K.recip(STAT[:, 48:64], STAT[:, 48:64])
            ON = KD
            K.tt("dve", ON, OACC, bc_last(STAT[:, 48:64], 128), ALU.mult)
            for half in range(2):
                hs = slice(half * 1024, (half + 1) * 1024)
                K.act(FB[1][:, hs], gps[half][:, :], AF.Silu)
            for g8 in range(2):
                pbank = PT[:, :] if g8 == 0 else PH_[:, :].bitcast(BF16)
                ptv8 = pbank.rearrange("p (q t) -> p q t", q=8)
                for q in range(8):
                    K.tr(ptv8[:, q, :], ON[:, g8 * 8 + q, :], IDB[:, :])
                K.stt("dve", Y[:, 2 + h, g8 * 1024:(g8 + 1) * 1024], pbank, PPT[:, 44:45],
                      FB[1][:, g8 * 1024:(g8 + 1) * 1024], ALU.mult, ALU.mult)


    def mixer_C(l, YC):
        w2 = w_in_d[l]
        QT = carve(HBO, [128, T], BF16)
        KT = carve(HBO + 4096, [128, T], BF16)
        ACC = carve(HBO + 8192, [128, 2, T], F32)
        VA = carve(HBO + 24576, [128, 16, 2, 66], BF16)
        PSB = [carve(FBO + 16384 + i * 1024, [128, 2, 2, 128], BF16) for i in range(2)]
        XB = carve(FBO, [128, T], BF16)
        PSB2 = [carve(FBO + 16384 + i * 1024, [128, 2, 256], BF16) for i in range(2)]
        AMASK2 = carve(HBO + 29056, [128, 256], BF16)
        K.cp("pool", AMASK2[:, 0:128], AMASK[:, 1, :])
        K.cp("pool", AMASK2[:, 128:256], AMASK[:, 0, :])
        RMB = carve(HBO + 28800, [128, 128], BF16)
        K.cp("dve", RMB, CST[:, 128:256])
        pn = 0
        for hp in range(2):
            K.memset("pool", ACC[0:65, :, :], 0.0)
            for p, dil in enumerate((1, 4, 16)):
                if p not in flags.get('pats', (0, 1, 2)):
                    continue
                L = T // dil
                nj = L // 128
                for which, dst in ((C_QC, QT), (C_KC, KT)):
                    wb = next_wb()
                    load_w(wb[:, :, 0:128], w2, which + p * 256 + hp * 128, 128)
                    for half in range(2):
                        hs = slice(half * 1024, (half + 1) * 1024)
                        ps = next_ps()
                        projT(wb[:, :, 0:128], ps, half * 1024, 1024)
                        K.cp("act", XB[:, hs], ps[:, :])
                    K.tt("dve", FB[1][:, :], XB[:, :], COS[:, :], ALU.mult)
                    for half in range(2):
                        hs = slice(half * 1024, (half + 1) * 1024)
                        ps = next_ps()
                        for b in range(2):
                            K.mm(ps[:, b * 512:(b + 1) * 512], RMB, XB[:, half * 1024 + b * 512:half * 1024 + (b + 1) * 512])
                        K.tt("dve", FB[2][:, hs], ps[:, :], SINS[:, hs], ALU.mult)
                    K.tt("dve", dst[:, :], FB[1][:, :], FB[2][:, :], ALU.add)
                if dbg is not None and l == 0 and hp == 0:
                    dbg_store(f"qt{p}", QT[:, :])
                    dbg_store(f"kt{p}", KT[:, :])
                if flags.get('cst', 9) < 2:
                    continue
                wb = next_wb()
                load_w(wb[:, :, 0:128], w2, C_VC + p * 256 + hp * 128, 128)
                K.memset("pool", VA[:, :, :, 64:65], 1.0)
                for g4 in range(4):
                    ps = next_ps()
                    for q in range(4):
                        idx = g4 * 4 + q
                        r, j = idx // nj, idx % nj
                        st = (128 * j) * dil + r
                        for kc in range(8):
                            K.mm(ps[:, q * 128:(q + 1) * 128], HT[:, kc, st:st + 127 * dil + 1:dil], wb[:, kc, 0:128],
                                 start=(kc == 0), stop=(kc == 7))
                    K.cp("act", VA[:, g4 * 4:(g4 + 1) * 4, :, 0:64],
                         ps[:, 0:512].rearrange("p (q h d) -> p q h d", q=4, h=2))
                if flags.get('cst', 9) < 3:
                    continue
                tiles = [(r, j) for r in range(dil) for j in range(nj)]
                pend = []

                def stage_a(r, j):
                    nonlocal pn
                    qlo, qhi = max(0, 128 * j - 64), min(L, 128 * j + 192)
                    nq = qhi - qlo
                    qoff = qlo - (128 * j - 64)
                    qsl = slice(qlo * dil + r, (qhi - 1) * dil + r + 1, dil)
                    ksl = slice(128 * j * dil + r, (128 * j + 127) * dil + r + 1, dil)
                    ps = next_ps()
                    pv3 = ps[:, :].rearrange("p (h c) -> p h c", h=2)
                    for hh in range(2):
                        K.mm(pv3[:, hh, qoff:qoff + nq], KT[hh * 64:(hh + 1) * 64, ksl], QT[hh * 64:(hh + 1) * 64, qsl])
                    pb = PSB2[pn % 2]
                    pn += 1
                    K.act(pb[:, :, qoff:qoff + nq], pv3[:, :, qoff:qoff + nq], AF.Exp, scale=0.125)
                    K.tt("dve", pb[:, :, qoff:qoff + nq], pb[:, :, qoff:qoff + nq],
                         bc_mid(AMASK2[:, qoff:qoff + nq], 2), ALU.mult)
                    return (r, j, nq, qoff, qsl, pv3, pb)

                def stage_b(st):
                    r, j, nq, qoff, qsl, pv3, pb = st
                    for hh in range(2):
                        K.mm(pv3[0:65, hh, 256 + qoff:256 + qoff + nq], VA[:, r * nj + j, hh, 0:65], pb[:, hh, qoff:qoff + nq])
                    K.tt("dve", ACC[0:65, :, qsl], ACC[0:65, :, qsl], pv3[0:65, :, 256 + qoff:256 + qoff + nq], ALU.add)

                for (r, j) in tiles:
                    pend.append(stage_a(r, j))
                    if len(pend) > 1:
                        stage_b(pend.pop(0))
                while pend:
                    stage_b(pend.pop(0))
            if flags.get('cst', 9) < 4:
                continue
            for hh in range(2):
                h = hp * 2 + hh
                K.act(FB[0][0:64, :], ACC[0:64, hh, :], AF.Square)
                for half in range(2):
                    hs = slice(half * 1024, (half + 1) * 1024)
                    ps = next_ps()
                    for b in range(2):
                        K.mm(ps[0:64, b * 512:(b + 1) * 512], SEL[0:65, :], ACC[0:65, hh, half * 1024 + b * 512:half * 1024 + (b + 1) * 512])
                    K.act(FB[1][0:64, hs], ps[0:64, :], AF.Square, scale=float(np.sqrt(EPS)))
                    ps2 = next_ps()
                    for b in range(2):
                        K.mm(ps2[0:64, b * 512:(b + 1) * 512], ONF[0:64, 0:64], FB[0][0:64, half * 1024 + b * 512:half * 1024 + (b + 1) * 512])
                    K.stt("dve", FB[1][0:64, hs], ps2[0:64, :], 1.0 / 64, FB[1][0:64, hs], ALU.mult, ALU.add)
                K.act(FB[1][0:64, :], FB[1][0:64, :], AF.Ln)
                K.act(FB[1][0:64, :], FB[1][0:64, :], AF.Exp, scale=-0.5)
                K.stt("dve", YC[0:64, h, :], ACC[0:64, hh, :], PPT[0:64, 45 + h:46 + h], FB[1][0:64, :], ALU.mult, ALU.mult)

    for l in range(DEPTH):
        K.dma("sp", PPT[:, :], pp_d[l])
        K.dma("sp", BCT[:, :], bc_d[l])
        if flags.get("A") or flags.get("B") or flags.get("C"):
            rmsnorm_to_HT(norm_mix_d[l, :])
        if flags.get("C"):
            YC = carve(RO, [128, 4, T], BF16)
            mixer_C(l, YC)
            dbg_store(f"yc{l}", YC[0:64, :, :])
            wo_pass(l, lambda c, i: YC[0:64, c, i * 128:(i + 1) * 128], 4, 64, 768)
        if flags.get("A") or flags.get("B"):
            Y = carve(RO, [128, 6, T], BF16)
            if flags.get("B"):
                mixer_B(l, Y)
            else:
                K.memset("pool", Y[:, 2:6, :], 0.0)
            if flags.get("A"):
                mixer_A(l, Y)
            else:
                K.memset("pool", Y[:, 0:2, :], 0.0)
            dbg_store(f"y{l}", Y[:, :, :])
            wo_pass(l, lambda c, i: Y[:, c, i * 128:(i + 1) * 128], 6, 128, 0)
        if flags.get("FFN"):
            ffn(l)

    K.dma("sp", GBC[:, :], norm_final_d[0, :].partition_broadcast(128))
    junk = carve(FBO, [128, D], BF16)
    for i in range(NT):
        K.act(junk, XT[:, i, :], AF.Square, accum=STAT[:, i:i + 1])
    K.ts("dve", STAT[:, 16:32], STAT[:, 0:16], 1.0 / D, EPS, ALU.mult, ALU.add)
    K.act(STAT[:, 16:32], STAT[:, 16:32], AF.Sqrt)
    K.recip(STAT[:, 32:48], STAT[:, 16:32])
    ob = [carve(FBO + 8192, [128, D], F32), carve(FBO + 8192 + 4096, [128, D], F32)]
    ov = out_d.rearrange("(n p) d -> p n d", p=128)
    for i in range(NT):
        o = ob[i % 2]
        K.stt("dve", o, XT[:, i, :], STAT[:, 32 + i:33 + i], GBC[:, :], ALU.mult, ALU.mult)
        K.dma("sp", ov[:, i, :], o)

    n = K.P.emit_all()
    print("instructions:", n)
    return nc, dbg_out


def make_consts():
    cst = np.zeros((128, 512), np.float32)
    inv = (np.float32(500000.0) ** (-np.arange(8, dtype=np.float32) / np.float32(8))).astype(np.float32)
    for p in range(128):
        j = p % 64
        if j < 16:
            cst[p, 0] = inv[j % 8]
            cst[p, 1] = -1.0 if j < 8 else 1.0
            src = p + 8 if j < 8 else p - 8
            cst[src, 128 + p] = 1.0
    return cst


def prep_inputs(inputs):
    f = lambda a: np.ascontiguousarray(np.asarray(a))
    x = f(inputs["x"])
    pos = f(inputs["positions"]).astype(np.int32)
    pp = np.zeros((DEPTH, 128, NPP), np.float32)
    conv_a = f(inputs["conv_a"]); norm_a = f(inputs["norm_a"]); conv_qkv = f(inputs["conv_qkv"])
    norm_dn = f(inputs["norm_dn"]); norm_c = f(inputs["norm_c"]); conv_ffn = f(inputs["conv_ffn"])
    for l in range(DEPTH):
        pp[l, :, 0:6] = conv_a[l].reshape(3, 2, 128).transpose(2, 1, 0).reshape(128, 6)
        pp[l, :, 6:8] = norm_a[l].reshape(2, 128).T
        pp[l, :, 8:44] = conv_qkv[l].reshape(3, 12, 128).transpose(2, 1, 0).reshape(128, 36)
        pp[l, :, 44] = norm_dn[l]
        pp[l, 0:64, 45:49] = norm_c[l].reshape(4, 64).T
        pp[l, :, 60:60 + 132] = conv_ffn[l].reshape(3, 44, 128).transpose(2, 1, 0).reshape(128, 132)
    bc = np.zeros((DEPTH, 128, 16), np.float32)
    for l in range(DEPTH):
        row = np.concatenate([f(inputs["a_log_f"])[l], f(inputs["a_log_b"])[l],
                              f(inputs["dt_bias_f"])[l], f(inputs["dt_bias_b"])[l]])
        bc[l] = np.broadcast_to(row[None, :], (128, 16))
    w_up_r = np.ascontiguousarray(
        f(inputs["w_up"]).reshape(DEPTH, 8, 128, 2, NFT, 128).transpose(0, 4, 2, 1, 3, 5)).reshape(DEPTH, NFT, 128, 2048)
    shared = dict(norm_mix=f(inputs["norm_mix"]), w_in=f(inputs["w_in"]), w_o=f(inputs["w_o"]),
                  norm_ffn=f(inputs["norm_ffn"]), w_up=w_up_r, w_down=f(inputs["w_down"]),
                  norm_final=f(inputs["norm_final"]).reshape(1, D), pp=pp, bc=bc, cst=make_consts())
    maps = []
    for b in range(x.shape[0]):
        m = dict(shared)
        m["x"] = x[b]
        m["pos"] = pos[b].reshape(1, T)
        maps.append(m)
    return maps


def kernel(**inputs):
    maps = prep_inputs(inputs)
    nc, _ = build()
    res = run_bass_kernel_spmd(nc, maps, core_ids=list(range(len(maps))))
    return np.stack([r["out"] for r in res.results], axis=0).astype(np.float32)
```

```python
import contextlib
import numpy as np
import concourse.bass as bass
import concourse.mybir as mybir
from concourse.bass_utils import run_bass_kernel_spmd

F32 = mybir.dt.float32
BF16 = mybir.dt.bfloat16
I32 = mybir.dt.int32
ALU = mybir.AluOpType
AF = mybir.ActivationFunctionType
AX = mybir.AxisListType

T = 2048
D = 1024
NT = 16
DEPTH = 2
DFF = 2816
NFT = 22
INW = 5136
EPS = 1e-6
C_XA, C_GB, C_GC = 0, 256, 512
C_QD, C_KD, C_VD, C_GD = 768, 1280, 1792, 2304
C_SC = 2816
C_QC, C_KC, C_VC = 2832, 3600, 4368
NPP = 192


class Prog:
    ENGS = ("pe", "act", "dve", "pool", "sp")
    EPOCH = 1000
    NDMA = 24

    def __init__(self, nc):
        self.nc = nc
        self.ins = []
        self.track = {}
        self.pstride = {}

    def _bbox(self, ap):
        t = ap.tensor
        name = t.name
        dims = ap.ap
        off = ap.offset
        isz = mybir.dt.size(ap.dtype)
        if "DRAM" in str(ap.space):
            lo = off
            hi = off + sum((c - 1) * abs(s) for s, c in dims) + 1
            return name, (0, 1, lo * isz, hi * isz)
        ps = int(np.prod(list(t.shape)[1:]))
        p0 = off // ps
        f0 = off % ps
        pstep, pcnt = dims[0]
        if pstep == 0:
            pcnt = 1
        f1 = f0 + sum((c - 1) * abs(s) for s, c in dims[1:]) + 1
        if "PSUM" in str(ap.space):
            return name, (0, 128, (f0 * isz) // 2048 * 2048, -((-f1 * isz) // 2048) * 2048)
        return name, (p0, p0 + pcnt, f0 * isz, f1 * isz)

    def add(self, eng, emit, reads=(), writes=(), dma=False):
        iid = len(self.ins)
        deps = set()
        rb = [self._bbox(a) for a in reads if "PSUM" not in str(a.space)]
        wb = [self._bbox(a) for a in writes] + [self._bbox(a) for a in reads if "PSUM" in str(a.space)]
        for name, bb in rb:
            for e in self.track.setdefault(name, []):
                if e[4] == "w" and e[0] < bb[1] and bb[0] < e[1] and e[2] < bb[3] and bb[2] < e[3]:
                    deps.add(e[5])
        for name, bb in wb:
            for e in self.track.setdefault(name, []):
                if e[0] < bb[1] and bb[0] < e[1] and e[2] < bb[3] and bb[2] < e[3]:
                    deps.add(e[5])
        for name, bb in wb:
            keep = []
            for e in self.track[name]:
                if bb[0] <= e[0] and e[1] <= bb[1] and bb[2] <= e[2] and e[3] <= bb[3]:
                    continue
                keep.append(e)
            keep.append([bb[0], bb[1], bb[2], bb[3], "w", iid])
            self.track[name] = keep
        for name, bb in rb:
            ents = self.track[name]
            found = False
            if not dma:
                for e in ents:
                    if e[4] == "r" and e[0] == bb[0] and e[1] == bb[1] and e[2] == bb[2] and e[3] == bb[3]:
                        if e[5] == iid:
                            found = True
                            break
                        o = self.ins[e[5]]
                        if o["eng"] == eng and not o["dma"]:
                            e[5] = iid
                            found = True
                            break
            if not found:
                ents.append([bb[0], bb[1], bb[2], bb[3], "r", iid])
        deps.discard(iid)
        self.ins.append(dict(eng=eng, emit=emit, deps=deps, dma=dma))
        return iid

    def emit_all(self):
        nc = self.nc
        ins = self.ins
        needed = set()
        for I in ins:
            nd = set()
            for d in I["deps"]:
                Dd = ins[d]
                if Dd["eng"] == "pe" and I["eng"] == "pe" and not Dd["dma"] and not I["dma"]:
                    continue
                nd.add(d)
            I["deps"] = nd
            needed |= nd
        cnt = {e: 0 for e in self.ENGS}
        dma_k = 0
        dma_val = [0] * self.NDMA
        for i, I in enumerate(ins):
            if I["dma"]:
                k = dma_k % self.NDMA
                dma_k += 1
                I["sig"] = ("dma", k, dma_val[k] + 16)
                I["prev"] = ("dma", k, dma_val[k])
                dma_val[k] += 16
            elif i in needed:
                c = cnt[I["eng"]]
                I["sig"] = (I["eng"], c // self.EPOCH, c % self.EPOCH + 1)
                cnt[I["eng"]] = c + 1
            else:
                I["sig"] = None
        print('signal counts', cnt, 'dmas', dma_k, 'per-engine instr', {e: sum(1 for I in ins if I['eng'] == e) for e in self.ENGS})
        nep = {e: cnt[e] // self.EPOCH + 1 for e in self.ENGS}
        with contextlib.ExitStack() as st:
            sems = {}
            for e in self.ENGS:
                for ep in range(nep[e]):
                    sems[(e, ep)] = st.enter_context(nc.semaphore(f"s_{e}_{ep}"))
            for k in range(min(self.NDMA, max(dma_k, 1))):
                sems[("dma", k)] = st.enter_context(nc.semaphore(f"s_dma_{k}"))
            block = st.enter_context(nc.Block())
            per_eng = {e: [I for I in ins if I["eng"] == e] for e in self.ENGS}

            def run(engname, engobj):
                waited = {}
                for I in per_eng[engname]:
                    ws = {}
                    for d in I["deps"]:
                        s = ins[d]["sig"]
                        key = (s[0], s[1])
                        if ws.get(key, 0) < s[2]:
                            ws[key] = s[2]
                    if I["dma"] and I["prev"][2] > 0:
                        p = I["prev"]
                        key = (p[0], p[1])
                        if ws.get(key, 0) < p[2]:
                            ws[key] = p[2]
                    for key, v in ws.items():
                        if waited.get(key, 0) >= v:
                            continue
                        waited[key] = v
                        engobj.wait_ge(sems[key], v)
                    bi = I["emit"](engobj)
                    s = I["sig"]
                    if s is not None:
                        bi.then_inc(sems[(s[0], s[1])], 16 if I["dma"] else 1)

            @block.tensor
            def _(e):
                run("pe", e)

            @block.scalar
            def _(e):
                run("act", e)

            @block.vector
            def _(e):
                run("dve", e)

            @block.gpsimd
            def _(e):
                run("pool", e)

            @block.sync
            def _(e):
                run("sp", e)
                for k in range(min(self.NDMA, dma_k)):
                    if dma_val[k] > 0:
                        e.wait_ge(sems[("dma", k)], dma_val[k])
        return len(ins)


class KB:
    def __init__(self, nc):
        self.nc = nc
        self.P = Prog(nc)
        self.rr = 0

    def mm(self, out, lhsT, rhs, start=True, stop=True):
        self.P.add("pe", lambda e: e.matmul(out, lhsT, rhs, start=start, stop=stop),
                   reads=[lhsT, rhs], writes=[out])

    def tr(self, out, in_, ident):
        self.P.add("pe", lambda e: e.transpose(out, in_, ident), reads=[in_, ident], writes=[out])

    def act(self, out, in_, func, scale=None, bias=None, accum=None, eng="act"):
        kw = {}
        rd = [in_]
        wr = [out]
        if scale is not None:
            kw["scale"] = scale
            if not isinstance(scale, (int, float)):
                rd.append(scale)
        if bias is not None:
            kw["bias"] = bias
            if not isinstance(bias, (int, float)):
                rd.append(bias)
        if accum is not None:
            kw["accum_out"] = accum
            wr.append(accum)
        self.P.add("act", lambda e: e.activation(out=out, in_=in_, func=func, **kw), reads=rd, writes=wr)

    def tt(self, eng, out, a, b, op):
        self.P.add(eng, lambda e: e.tensor_tensor(out=out, in0=a, in1=b, op=op), reads=[a, b], writes=[out])

    def ts(self, eng, out, a, s1, s2, op0, op1=None):
        rd = [a]
        if not isinstance(s1, (int, float)):
            rd.append(s1)
        if s2 is not None and not isinstance(s2, (int, float)):
            rd.append(s2)
        if op1 is None:
            self.P.add(eng, lambda e: e.tensor_scalar(out=out, in0=a, scalar1=s1, scalar2=None, op0=op0),
                       reads=rd, writes=[out])
        else:
            self.P.add(eng, lambda e: e.tensor_scalar(out=out, in0=a, scalar1=s1, scalar2=s2, op0=op0, op1=op1),
                       reads=rd, writes=[out])

    def stt(self, eng, out, a, scalar, b, op0, op1):
        rd = [a, b]
        if not isinstance(scalar, (int, float)):
            rd.append(scalar)
        self.P.add(eng, lambda e: e.scalar_tensor_tensor(out=out, in0=a, scalar=scalar, in1=b, op0=op0, op1=op1),
                   reads=rd, writes=[out])

    def cp(self, eng, out, in_):
        if eng == "act":
            self.act(out, in_, AF.Copy)
        else:
            self.P.add(eng, lambda e: e.tensor_copy(out=out, in_=in_), reads=[in_], writes=[out])

    def recip(self, out, in_):
        self.P.add("dve", lambda e: e.reciprocal(out=out, in_=in_), reads=[in_], writes=[out])

    def memset(self, eng, out, val):
        self.P.add(eng, lambda e: e.memset(out, val), writes=[out])

    def aselect(self, out, in_, pattern, op, fill, base, cm):
        self.P.add("pool", lambda e: e.affine_select(out=out, in_=in_, pattern=pattern, compare_op=op,
                                                     fill=fill, base=base, channel_multiplier=cm),
                   reads=[in_], writes=[out])

    def dma(self, eng, out, in_):
        self.P.add(eng, lambda e: e.dma_start(out=out, in_=in_), reads=[in_], writes=[out], dma=True)


def build(flags=None, dbg=None):
    flags = flags or dict(A=True, B=True, C=True, FFN=True)
    nc = bass.Bass("TRN2", target_bir_lowering=False)
    K = KB(nc)

    def din(name, shape, dt=F32):
        return nc.dram_tensor(name, list(shape), dt, kind="ExternalInput").ap()

    x_d = din("x", [T, D])
    pos_d = din("pos", [1, T], I32)
    norm_mix_d = din("norm_mix", [DEPTH, D])
    w_in_d = din("w_in", [DEPTH, D, INW])
    w_o_d = din("w_o", [DEPTH, D, D])
    norm_ffn_d = din("norm_ffn", [DEPTH, D])
    w_up_d = din("w_up", [DEPTH, NFT, 128, 8 * 256])
    w_down_d = din("w_down", [DEPTH, DFF, D])
    norm_final_d = din("norm_final", [1, D])
    pp_d = din("pp", [DEPTH, 128, NPP])
    bc_d = din("bc", [DEPTH, 128, 16])
    cst_d = din("cst", [128, 512])
    out_d = nc.dram_tensor("out", [T, D], F32, kind="ExternalOutput").ap()
    dbg_out = {}

    def dbg_store(name, ap, eng="sp"):
        if dbg is None or name not in dbg:
            return
        o = nc.dram_tensor("dbg_" + name, list(ap.shape), ap.dtype, kind="ExternalOutput").ap()
        dbg_out[name] = o
        K.dma(eng, o, ap)

    XT = nc.alloc_sbuf_tensor("XT", [128, NT, D], F32)
    HT = nc.alloc_sbuf_tensor("HT", [128, 8, T], BF16)
    SCR_B = 90 * 1024
    SCR = nc.alloc_sbuf_tensor("SCR", [128, SCR_B // 2], BF16)
    PPT = nc.alloc_sbuf_tensor("PPT", [128, NPP], F32)
    BCT = nc.alloc_sbuf_tensor("BCT", [128, 16], F32)
    CST = nc.alloc_sbuf_tensor("CST", [128, 512], F32)
    GBC = nc.alloc_sbuf_tensor("GBC", [128, D], F32)
    IDB = nc.alloc_sbuf_tensor("IDB", [128, 128], BF16)
    IDF = nc.alloc_sbuf_tensor("IDF", [128, 128], F32)
    ONF = nc.alloc_sbuf_tensor("ONF", [128, 128], F32)
    ONB = nc.alloc_sbuf_tensor("ONB", [128, 128], BF16)
    STAT = nc.alloc_sbuf_tensor("STAT", [128, 64], F32)
    EPSC = nc.alloc_sbuf_tensor("EPSC", [128, 2], F32)
    BLK64 = nc.alloc_sbuf_tensor("BLK64", [128, 128], F32)
    SEL = nc.alloc_sbuf_tensor("SEL", [65, 64], F32)
    IDF2 = nc.alloc_sbuf_tensor("IDF2", [128, 2, 128], F32)
    AMASK = nc.alloc_sbuf_tensor("AMASK", [128, 2, 128], BF16)
    COS = nc.alloc_sbuf_tensor("COS", [128, T], BF16)
    TAB = nc.alloc_sbuf_tensor("TAB", [128, 6, 128], F32)
    SINS = nc.alloc_sbuf_tensor("SINS", [128, T], BF16)
    PS = [nc.alloc_psum_tensor(f"PS{i}", [128, 1024], F32) for i in range(3)]
    PT = nc.alloc_psum_tensor("PT", [128, 1024], BF16)
    PH = nc.alloc_psum_tensor("PH", [128, 512], F32)
    PH_ = PH
    print("sbuf bytes remaining", nc.sbuf_bytes_remaining)

    def carve(off, shape, dt):
        isz = mybir.dt.size(dt)
        n = int(np.prod(shape[1:]))
        assert off % 4 == 0 and off + n * isz <= SCR_B, (off, shape)
        v = SCR[:, off // 2: off // 2 + n * isz // 2]
        if dt != BF16:
            v = v.bitcast(dt)
        if len(shape) == 3:
            v = v.rearrange("p (a b) -> p a b", a=shape[1])
        elif len(shape) == 4:
            v = v.rearrange("p (a b c) -> p a b c", a=shape[1], b=shape[2])
        return v

    WBO = 0
    WB = [carve(WBO + i * 4096, [128, 8, 256], BF16) for i in range(3)]
    FBO = 12288
    FB = [carve(FBO + i * 8192, [128, T], F32) for i in range(3)]
    RO = FBO + 3 * 8192

    psn = [0]

    def next_ps():
        p = PS[psn[0] % 3]
        psn[0] += 1
        return p

    wbn = [0]

    def next_wb():
        w = WB[wbn[0] % 3]
        wbn[0] += 1
        return w

    K.dma("sp", CST[:, :], cst_d[:, :])
    K.memset("pool", IDF[:, :], 1.0)
    K.aselect(IDF[:, :], IDF[:, :], [[-1, 128]], ALU.is_equal, 0.0, 0, 1)
    K.cp("pool", IDB[:, :], IDF[:, :])
    K.memset("pool", ONF[:, :], 1.0)
    K.memset("pool", ONB[:, :], 1.0)
    K.memset("pool", EPSC[:, 0:1], EPS)
    K.memset("pool", EPSC[:, 1:2], -float(np.pi))
    K.memset("pool", BLK64[:, :], 0.0)
    K.memset("pool", BLK64[0:64, 0:64], 1.0)
    K.memset("pool", BLK64[64:128, 64:128], 1.0)
    K.memset("pool", SEL[:, :], 0.0)
    K.memset("pool", SEL[64:65, :], 1.0)
    K.memset("pool", IDF2[:, :, :], 1.0)
    K.aselect(IDF2[:, 0, :], IDF2[:, 0, :], [[-1, 128]], ALU.is_ge, 0.0, 0, 1)
    K.aselect(IDF2[:, 1, :], IDF2[:, 1, :], [[1, 128]], ALU.is_ge, 0.0, 0, -1)
    K.cp("pool", AMASK[:, :, :], IDF2[:, :, :])
    posi = carve(FBO, [128, T], I32)
    K.dma("sp", posi, pos_d[0, :].partition_broadcast(128))
    posf = FB[1]
    K.cp("dve", posf[:, :], posi)
    TWO_PI = 2.0 * float(np.pi)

    def sin_table(dst, phase, scale_ap):
        x = FB[2]
        if phase == 0.0:
            K.ts("dve", x[:, :], posf[:, :], CST[:, 0:1], None, ALU.mult)
        else:
            K.ts("dve", x[:, :], posf[:, :], CST[:, 0:1], phase, ALU.mult, ALU.add)
        kf = FB[0]
        K.ts("dve", kf[:, :], x[:, :], 1.0 / TWO_PI, None, ALU.mult)
        ki = carve(FBO, [128, T], I32)
        K.cp("dve", ki, kf[:, :])
        K.cp("dve", kf[:, :], ki)
        K.stt("dve", x[:, :], kf[:, :], -TWO_PI, x[:, :], ALU.mult, ALU.add)
        K.ts("dve", kf[:, :], x[:, :], float(np.pi), TWO_PI, ALU.is_gt, ALU.mult)
        K.tt("dve", x[:, :], x[:, :], kf[:, :], ALU.subtract)
        K.act(kf[:, :], x[:, :], AF.Sin)
        if scale_ap is None:
            K.cp("dve", dst, kf[:, :])
        else:
            K.ts("dve", dst, kf[:, :], scale_ap, None, ALU.mult)

    sin_table(SINS[:, :], 0.0, CST[:, 1:2])
    sin_table(COS[:, :], 0.5 * float(np.pi), None)
    dbg_store("cos", COS[:, :])
    dbg_store("sins", SINS[:, :])

    xv = x_d.rearrange("(n p) d -> p n d", p=128)
    for i in range(NT):
        K.dma("sp" if i % 2 == 0 else "act", XT[:, i, :], xv[:, i, :])

    elt = [0]

    def rr2():
        elt[0] += 1
        return "dve" if elt[0] % 2 else "pool"

    def rmsnorm_to_HT(g_row):
        K.dma("sp", GBC[:, :], g_row.partition_broadcast(128))
        junk = carve(FBO, [128, D], BF16)
        hn = [carve(FBO + 8192, [128, D], BF16), carve(FBO + 8192 + 2048, [128, D], BF16)]
        junk2 = carve(FBO + 2048, [128, D], BF16)
        for i in range(NT):
            if i % 2 == 0:
                K.act(junk, XT[:, i, :], AF.Square, accum=STAT[:, i:i + 1])
            else:
                K.P.add("dve", lambda e, i=i: e.scalar_tensor_tensor(out=junk2, in0=XT[:, i, :], scalar=1.0, in1=XT[:, i, :],
                                                                      op0=ALU.mult, op1=ALU.mult, accum_out=STAT[:, i:i + 1]),
                        reads=[XT[:, i, :]], writes=[junk2, STAT[:, i:i + 1]])
        K.ts("dve", STAT[:, 16:32], STAT[:, 0:16], 1.0 / D, EPS, ALU.mult, ALU.add)
        K.act(STAT[:, 16:32], STAT[:, 16:32], AF.Sqrt)
        K.recip(STAT[:, 32:48], STAT[:, 16:32])
        for i in range(NT):
            h = hn[i % 2]
            K.stt("dve", h, XT[:, i, :], STAT[:, 32 + i:33 + i], GBC[:, :], ALU.mult, ALU.mult)
            pbank = PT[:, :] if i % 2 == 0 else PH_[:, :].bitcast(BF16)
            ptv = pbank.rearrange("p (c t) -> p c t", c=8)
            for c in range(8):
                K.tr(ptv[:, c, :], h[:, c * 128:(c + 1) * 128], IDB[:, :])
            K.cp("act", HT[:, :, i * 128:(i + 1) * 128], ptv)

    def load_w(dst, w2d, c0, n, k0=0, kn=8):
        src = w2d[k0 * 128:(k0 + kn) * 128, c0:c0 + n].rearrange("(k p) c -> p k c", p=128)
        K.dma("pool", dst, src)

    def projT(wt, ps, t0, tn):
        for b0 in range(0, tn, 512):
            bn = min(512, tn - b0)
            for kc in range(8):
                K.mm(ps[:, b0:b0 + bn], wt[:, kc, :], HT[:, kc, t0 + b0:t0 + b0 + bn], start=(kc == 0), stop=(kc == 7))

    def ffn(l):
        PHS = [PH_, PT[:, :].bitcast(F32)]
        rmsnorm_to_HT(norm_ffn_d[l, :])
        TB = 512
        stg = [carve(FBO + i * 2048, [128, TB], F32) for i in range(6)]
        actT = carve(FBO + 12288, [128, NFT, TB], BF16)
        WD = carve(FBO + 12288 + 22528, [128, NFT, D], BF16)
        wd2 = w_down_d[l]
        for ft in range(NFT):
            K.dma("pool", WD[:, ft, :], wd2[ft * 128:(ft + 1) * 128, :])
        wu2 = w_up_d[l]
        sn = 0
        ntb = flags.get('tbl', T // TB)
        nft = flags.get('ftl', NFT)
        units = [(tb, ft) for tb in range(ntb) for ft in range(nft)]
        wbs = {}

        def issue(u):
            wbs[u] = WB[u % 3]
            K.dma("pool", wbs[u][:, :, :], wu2[units[u][1]].rearrange("p (k c) -> p k c", k=8))

        issue(0)
        if len(units) > 1:
            issue(1)
        for u, (tb, ft) in enumerate(units):
            t0 = tb * TB
            if True:
                if u + 2 < len(units):
                    issue(u + 2)
                wb = wbs[u]
                ps = next_ps()
                cs = []
                for half in range(2):
                    wt = wb[:, :, half * 128:(half + 1) * 128]
                    pv = ps[:, half * 512:(half + 1) * 512]
                    projT(wt, pv, t0, TB)
                    hcol = (ft % 8) * 4 + half * 2
                    PH = PHS[u % 2]
                    if 0 < tb < T // TB - 1:
                        for kc in range(8):
                            K.mm(PH[:, hcol:hcol + 2], wt[:, kc, :], HT[:, kc, t0 - 1:t0 + TB + 1:TB + 1], start=(kc == 0), stop=(kc == 7))
                    elif tb > 0:
                        for kc in range(8):
                            K.mm(PH[:, hcol:hcol + 1], wt[:, kc, :], HT[:, kc, t0 - 1:t0], start=(kc == 0), stop=(kc == 7))
                    else:
                        for kc in range(8):
                            K.mm(PH[:, hcol + 1:hcol + 2], wt[:, kc, :], HT[:, kc, t0 + TB:t0 + TB + 1], start=(kc == 0), stop=(kc == 7))
                    col = 60 + (half * NFT + ft) * 3
                    w0, w1, w2 = PPT[:, col:col + 1], PPT[:, col + 1:col + 2], PPT[:, col + 2:col + 3]
                    c = stg[sn % 6]
                    sn += 1
                    K.act(c, pv, AF.Copy, scale=w1)
                    K.stt("dve", c[:, 1:TB], pv[:, 0:TB - 1], w0, c[:, 1:TB], ALU.mult, ALU.add)
                    K.stt("dve", c[:, 0:TB - 1], pv[:, 1:TB], w2, c[:, 0:TB - 1], ALU.mult, ALU.add)
                    if tb > 0:
                        K.stt("dve", c[:, 0:1], PH[:, hcol:hcol + 1], w0, c[:, 0:1], ALU.mult, ALU.add)
                    if tb < T // TB - 1:
                        K.stt("dve", c[:, TB - 1:TB], PH[:, hcol + 1:hcol + 2], w2, c[:, TB - 1:TB], ALU.mult, ALU.add)
                    cs.append(c)
                K.act(cs[0], cs[0], AF.Silu)
                K.tt("pool", actT[:, ft, :], cs[0], cs[1], ALU.mult)
            if ft != nft - 1:
                continue
            for ti in range(TB // 128):
                tile_i = tb * (TB // 128) + ti
                ps = next_ps()
                for ch in range(2):
                    nf = flags.get('ftl', NFT)
                    for ft in range(nf):
                        K.mm(ps[:, ch * 512:(ch + 1) * 512], actT[:, ft, ti * 128:(ti + 1) * 128],
                             WD[:, ft, ch * 512:(ch + 1) * 512], start=(ft == 0), stop=(ft == nf - 1))
                K.tt("dve", XT[:, tile_i, :], XT[:, tile_i, :], ps[:, :], ALU.add)


    def wo_pass(l, lhs_fn, nchunks, kparts, row0):
        WO = carve(WBO, [128, nchunks, D], BF16)
        for c in range(nchunks):
            K.dma("pool", WO[0:kparts, c, :], w_o_d[l][row0 + c * kparts:row0 + (c + 1) * kparts, :])
        for i in range(NT):
            ps = next_ps()
            for ch in range(2):
                for c in range(nchunks):
                    K.mm(ps[:, ch * 512:(ch + 1) * 512], lhs_fn(c, i), WO[0:kparts, c, ch * 512:(ch + 1) * 512],
                         start=(c == 0), stop=(c == nchunks - 1))
            K.tt("dve", XT[:, i, :], XT[:, i, :], ps[:, :], ALU.add)

    def mixer_A(l, Y):
        w2 = w_in_d[l]
        for ct in range(2):
            wts = {}
            for name, c0 in (("xa", C_XA), ("gc", C_GC), ("gb", C_GB)):
                wb = next_wb()
                load_w(wb[:, :, 0:128], w2, c0 + ct * 128, 128)
                wts[name] = wb[:, :, 0:128]
            for half in range(2):
                hs = slice(half * 1024, (half + 1) * 1024)
                ps = next_ps()
                projT(wts["xa"], ps, half * 1024, 1024)
                K.cp("act", FB[0][:, hs], ps[:, :])
            for half in range(2):
                hs = slice(half * 1024, (half + 1) * 1024)
                ps = next_ps()
                projT(wts["gc"], ps, half * 1024, 1024)
                K.tt("dve", FB[1][:, hs], ps[:, :], FB[0][:, hs], ALU.mult)
            w0, w1, w2c = (PPT[:, ct * 3 + k:ct * 3 + k + 1] for k in range(3))
            K.ts("dve", FB[2][:, :], FB[1][:, :], w1, None, ALU.mult)
            K.stt("dve", FB[2][:, 1:T], FB[1][:, 0:T - 1], w0, FB[2][:, 1:T], ALU.mult, ALU.add)
            K.stt("dve", FB[2][:, 0:T - 1], FB[1][:, 1:T], w2c, FB[2][:, 0:T - 1], ALU.mult, ALU.add)
            for half in range(2):
                hs = slice(half * 1024, (half + 1) * 1024)
                ps = next_ps()
                projT(wts["gb"], ps, half * 1024, 1024)
                K.tt("dve", FB[0][:, hs], ps[:, :], FB[2][:, hs], ALU.mult)
            K.act(FB[1][:, :], FB[0][:, :], AF.Square)
            for half in range(2):
                hs = slice(half * 1024, (half + 1) * 1024)
                ps = next_ps()
                for b in range(2):
                    K.mm(ps[:, b * 512:(b + 1) * 512], BLK64[:, :], FB[1][:, half * 1024 + b * 512:half * 1024 + (b + 1) * 512])
                K.act(FB[2][:, hs], ps[:, :], AF.Ln, scale=1.0 / 64, bias=EPSC[:, 0:1])
            K.act(FB[2][:, :], FB[2][:, :], AF.Exp, scale=-0.5)
            K.stt("dve", Y[:, ct, :], FB[0][:, :], PPT[:, 6 + ct:7 + ct], FB[2][:, :], ALU.mult, ALU.mult)

    HBO = RO + 24576

    def bc_last(ap, n):
        return bass.AP(ap.tensor, ap.offset, [list(d) for d in ap.ap] + [[0, n]])

    def bc_mid(ap, n):
        return bass.AP(ap.tensor, ap.offset, [list(ap.ap[0]), [0, n]] + [list(d) for d in ap.ap[1:]])

    def mixer_B(l, Y):
        w2 = w_in_d[l]
        QTn = carve(HBO, [128, T], BF16)
        KTn = carve(HBO + 4096, [128, T], BF16)
        VT = carve(HBO + 8192, [128, T], BF16)
        KD = carve(HBO + 12288, [128, NT, 128], BF16)
        VB = carve(HBO + 16384, [128, NT, 128], BF16)
        QG = carve(HBO + 20480, [128, T], BF16)
        tmpq = [carve(HBO + 24576 + i * 1024, [128, 4, 128], BF16) for i in range(6)]
        F2 = FBO + 16384
        ND = carve(F2, [128, 4, 128], F32)
        DM = carve(F2 + 2048, [128, 4, 128], F32)
        QKT = carve(F2 + 4096, [128, 4, 128], BF16)
        RR = carve(F2 + 5120, [128, 128], BF16)
        VN = carve(F2 + 5376, [128, 128], BF16)
        SBF = carve(F2 + 5632, [128, 128], BF16)
        S32 = carve(F2 + 5888, [128, 128], F32)
        STRICT = carve(F2 + 6400, [128, 2, 128], BF16)
        A0 = carve(F2 + 6912, [128, 4, 128], BF16)
        F1 = FBO + 8192
        UUS = [carve(F1 + i * 1024, [128, 4, 128], BF16) for i in range(2)]
        PHIS = [carve(F1 + 2048 + i * 1024, [128, 4, 128], BF16) for i in range(2)]
        PSIS = [carve(F1 + 4096 + i * 1024, [128, 4, 128], BF16) for i in range(2)]
        QKTS = [carve(F1 + 6144 + i * 1024, [128, 4, 128], BF16) for i in range(2)]
        SSV = [carve(F2 + 4352 + i * 256, [128, 128], BF16) for i in range(4)] + \
              [carve(RO + 7168 + i * 256, [128, 128], BF16) for i in range(4)]
        PTF = PT[:, :].bitcast(F32)
        TQS = [tmpq + [A0], [carve(RO + i * 1024, [128, 4, 128], BF16) for i in range(7)]]
        OACC = FB[0].rearrange("p (i d) -> p i d", i=NT)
        BETA, GC, EGC, EKD, EG, NBG = (TAB[:, k, :] for k in range(6))

        def col(d, h, i):
            return d * 64 + h * 16 + i

        K.cp("pool", STRICT[:, 0, :], IDF2[:, 0, :])
        K.aselect(STRICT[:, 0, :], STRICT[:, 0, :], [[-1, 128]], ALU.not_equal, 0.0, 0, 1)
        K.cp("pool", STRICT[:, 1, :], IDF2[:, 1, :])
        K.aselect(STRICT[:, 1, :], STRICT[:, 1, :], [[-1, 128]], ALU.not_equal, 0.0, 0, 1)

        wb = next_wb()
        load_w(wb[:, :, 0:16], w2, C_SC, 16)
        for i in range(NT):
            for kc in range(8):
                K.mm(PH[:, i * 16:(i + 1) * 16], HT[:, kc, i * 128:(i + 1) * 128], wb[:, kc, 0:16],
                     start=(kc == 0), stop=(kc == 7))
        SC = FB[1][:, 0:256].rearrange("p (i c) -> p i c", i=NT)
        K.cp("act", SC, PH[:, 0:256].rearrange("p (i c) -> p i c", i=NT))
        T8 = lambda ap: ap.rearrange("p (c i) -> p c i", c=8)
        K.act(T8(BETA), SC[:, :, 0:8].rearrange("p i c -> p c i"), AF.Sigmoid)
        XA = FB[1][:, 256:384]
        K.tt("dve", T8(XA), SC[:, :, 8:16].rearrange("p i c -> p c i"), bc_last(BCT[:, 8:16], NT), ALU.add)
        K.act(XA, XA, AF.Exp)
        K.act(XA, XA, AF.Ln, bias=ONF[:, 0:1])
        EA = FB[1][:, 384:392]
        K.act(EA, BCT[:, 0:8], AF.Exp)
        G = FB[1][:, 512:640]
        K.stt("dve", T8(G), T8(XA), -1.0, bc_last(EA, NT), ALU.mult, ALU.mult)
        K.mm(PH[:, 0:64], IDF2[:, 1, :], G[:, 0:64])
        K.mm(PH[:, 64:128], IDF2[:, 0, :], G[:, 64:128])
        K.mm(PH[:, 128:256], ONF[:, :], G[:, 0:128])
        K.cp("dve", GC, PH[:, 0:128])
        K.act(EGC, PH[:, 0:128], AF.Exp)
        K.act(EG, PH[:, 128:256], AF.Exp)
        K.tt("dve", EKD, PH[:, 128:256], GC, ALU.subtract)
        K.act(EKD, EKD, AF.Exp)
        K.stt("dve", NBG, BETA, -1.0, EGC, ALU.mult, ALU.mult)
        GCH = carve(F2 + 5376, [128, 128], BF16)
        GCL = carve(F2 + 5632, [128, 128], BF16)
        K.cp("act", GCH, GC)
        GTMP = FB[1][:, 1024:1152]
        K.tt("dve", GTMP, GC, GCH, ALU.subtract)
        K.cp("act", GCL, GTMP)
        EGCB = carve(F2 + 7936, [128, 128], BF16)
        K.cp("act", EGCB, EGC)
        if l == 0:
            dbg_store("tab", TAB[:, :, :])

        for h in range(4):
            X3 = carve(HBO + 12288, [128, T], F32)
            EDGE = carve(F2 + 4096, [128, 8], F32)

            def prep_chain(wi, c0w, dst, cb):
                Pp = PS[wi]
                wb = WB[wi]
                load_w(wb[:, :, 0:128], w2, c0w + h * 128, 128)
                pc = 8 + (wi * 4 + h) * 3
                w0, w1, w2c = (PPT[:, pc + k:pc + k + 1] for k in range(3))
                for half in range(2):
                    o = half * 1024
                    projT(wb[:, :, 0:128], Pp, o, 1024)
                    K.act(cb[:, o:o + 1024], Pp[:, :], AF.Copy, scale=w1)
                    yield
                    K.stt("dve", cb[:, o + 1:o + 1024], Pp[:, 0:1023], w0, cb[:, o + 1:o + 1024], ALU.mult, ALU.add)
                    yield
                    K.stt("dve", cb[:, o:o + 1023], Pp[:, 1:1024], w2c, cb[:, o:o + 1023], ALU.mult, ALU.add)
                    if half == 0:
                        K.cp("act", EDGE[:, wi * 2:wi * 2 + 1], Pp[:, 1023:1024])
                    else:
                        K.cp("act", EDGE[:, wi * 2 + 1:wi * 2 + 2], Pp[:, 0:1])
                    yield
                K.stt("dve", cb[:, 1023:1024], EDGE[:, wi * 2 + 1:wi * 2 + 2], w2c, cb[:, 1023:1024], ALU.mult, ALU.add)
                K.stt("dve", cb[:, 1024:1025], EDGE[:, wi * 2:wi * 2 + 1], w0, cb[:, 1024:1025], ALU.mult, ALU.add)
                K.act(cb[:, :], cb[:, :], AF.Silu)
                yield
                if wi == 2:
                    K.cp("dve", dst[:, :], cb[:, :])
                    return
                K.act(dst[:, :], cb[:, :], AF.Square)
                yield
                for half in range(2):
                    o = half * 1024
                    for b2 in range(2):
                        K.mm(Pp[:, b2 * 512:(b2 + 1) * 512], ONB[:, :], dst[:, o + b2 * 512:o + (b2 + 1) * 512])
                    K.act(Pp[:, :], Pp[:, :], AF.Ln, bias=EPSC[:, 0:1])
                    yield
                    K.act(dst[:, o:o + 1024], Pp[:, :], AF.Exp, scale=-0.5)
                    yield
                K.stt("dve", dst[:, :], cb[:, :], (128.0 ** -0.5) if wi == 0 else 1.0, dst[:, :], ALU.mult, ALU.mult)
                yield

            chains = [prep_chain(0, C_QD, QTn, FB[0]), prep_chain(1, C_KD, KTn, FB[1]), prep_chain(2, C_VD, VT, X3)]
            while chains:
                for g in list(chains):
                    try:
                        next(g)
                    except StopIteration:
                        chains.remove(g)
            gwb = WB[0]
            load_w(gwb[:, :, 0:128], w2, C_GD + h * 128, 128)
            if l == 0 and h == 0:
                dbg_store("qtn", QTn[:, :])
                dbg_store("ktn", KTn[:, :])
                dbg_store("vt", VT[:, :])

            for d in range(2):
                c0 = col(d, h, 0)
                for src, dstt, tab in ((KTn, KD, EKD), (VT, VB, BETA)):
                    for g8 in range(2):
                        ps = next_ps()
                        for q in range(8):
                            i = g8 * 8 + q
                            K.mm(ps[:, q * 128:(q + 1) * 128], src[:, i * 128:(i + 1) * 128], IDB[:, :])
                        K.tt("dve", dstt[:, g8 * 8:(g8 + 1) * 8, :], ps[:, :].rearrange("p (q d) -> p q d", q=8),
                             bc_last(tab[:, c0 + g8 * 8:c0 + (g8 + 1) * 8], 128), ALU.mult)
                for g8 in range(2):
                    ps = next_ps()
                    for q in range(8):
                        i = g8 * 8 + q
                        K.mm(ps[:, q * 128:(q + 1) * 128],
                             bass.AP(EGCB.tensor, EGCB[:, c0 + i:c0 + i + 1].offset, [list(EGCB.ap[0]), [0, 128]]), IDB[:, :])
                    K.tt("dve", QG[:, g8 * 1024:(g8 + 1) * 1024], ps[:, :], QTn[:, g8 * 1024:(g8 + 1) * 1024], ALU.mult)
                K.memset("pool", S32[:, :], 0.0)
                K.memset("pool", SSV[0][:, :], 0.0)
                quads = list(range(4)) if d == 0 else list(range(3, -1, -1))
                ipn = [0]

                def inv_ps():
                    ipn[0] += 1
                    return PS[ipn[0] % 2]

                def inv_gen(qd, par, tset):
                    i0 = qd * 4
                    QKTo = QKTS[par]
                    tmpq = TQS[tset][0:6]
                    A0 = TQS[tset][6]
                    Dm = (ND, DM)[tset]
                    PQ = PS[tset]
                    MB = PS[2][:, tset * 512:(tset + 1) * 512]
                    psr = MB
                    for q in range(4):
                        cc = c0 + i0 + q
                        K.mm(psr[:, q * 128:(q + 1) * 128],
                             bass.AP(GCH.tensor, GCH[:, cc:cc + 1].offset, [list(GCH.ap[0]), [0, 128]]), IDB[:, :],
                             start=True, stop=False)
                        K.mm(psr[:, q * 128:(q + 1) * 128],
                             bass.AP(GCL.tensor, GCL[:, cc:cc + 1].offset, [list(GCL.ap[0]), [0, 128]]), IDB[:, :],
                             start=False, stop=True)
                    psk = PQ
                    for q in range(4):
                        ts_ = slice((i0 + q) * 128, (i0 + q + 1) * 128)
                        K.mm(psk[:, q * 128:(q + 1) * 128], KTn[:, ts_], KTn[:, ts_])
                    for q in range(4):
                        ts_ = slice((i0 + q) * 128, (i0 + q + 1) * 128)
                        K.mm(psk[:, 512 + q * 128:512 + (q + 1) * 128], QTn[:, ts_], KTn[:, ts_])
                    K.tt("dve", Dm, psr[:, 0:512].rearrange("p (q s) -> p q s", q=4),
                         bc_last(GC[:, c0 + i0:c0 + i0 + 4], 128), ALU.subtract)
                    yield
                    K.act(Dm, Dm, AF.Exp, scale=-1.0)
                    yield
                    K.stt("dve", Dm, Dm, 1.0, bc_mid(IDF2[:, d, :], 4), ALU.min, ALU.mult)
                    TQ = tmpq[5]
                    K.tt("dve", TQ, psk[:, 512:1024].rearrange("p (q s) -> p q s", q=4), Dm, ALU.mult)
                    yield
                    K.tt("pool", Dm, Dm, bc_mid(STRICT[:, d, :], 4), ALU.mult)
                    K.tt("pool", Dm, Dm, bc_last(BETA[:, c0 + i0:c0 + i0 + 4], 128), ALU.mult)
                    yield
                    K.tt("dve", A0, psk[:, 0:512].rearrange("p (q s) -> p q s", q=4), Dm, ALU.mult)
                    yield
                    ptv = PT[:, :].rearrange("p (a q s) -> p a q s", a=2, q=4)
                    for q in range(4):
                        K.tr(ptv[:, 0, q, :], A0[:, q, :], IDB[:, :])
                    for q in range(4):
                        K.tr(ptv[:, 1, q, :], TQ[:, q, :], IDB[:, :])
                    P0 = tmpq[1]
                    K.cp("act", P0, ptv[:, 0, :, :])
                    K.cp("act", QKTo, ptv[:, 1, :, :])
                    M0 = tmpq[2]
                    K.tt("pool", M0, bc_mid(IDB[:, :], 4), P0, ALU.subtract)
                    yield
                    Pc, Qc, Mc = P0, A0, M0
                    free = [tmpq[3], tmpq[4], tmpq[5], tmpq[0]]
                    for lev in range(1, 6):
                        psp = PQ
                        last = (lev == 5)
                        for q in range(4):
                            K.mm(psp[:, 512 + q * 128:512 + (q + 1) * 128], Pc[:, q, :], Qc[:, q, :])
                        if not last:
                            for q in range(4):
                                K.mm(psp[:, q * 128:(q + 1) * 128], Qc[:, q, :], Pc[:, q, :])
                        Qn = free.pop(0)
                        K.cp("act", Qn, psp[:, 512:1024].rearrange("p (q s) -> p q s", q=4))
                        if not last:
                            Pn = free.pop(0)
                            K.cp("act", Pn, psp[:, 0:512].rearrange("p (q s) -> p q s", q=4))
                        yield
                        psm = MB
                        for q in range(4):
                            K.mm(psm[:, q * 128:(q + 1) * 128], Qn[:, q, :], Mc[:, q, :])
                        Mn = free.pop(0)
                        K.tt("dve", Mn, psm[:, 0:512].rearrange("p (q s) -> p q s", q=4), Mc, ALU.add)
                        free.append(Mc)
                        if Qc is not A0:
                            free.append(Qc)
                        if not last:
                            free.append(Pc)
                            Pc = Pn
                        Qc, Mc = Qn, Mn
                        yield
                    avail = [t for t in TQS[tset] if t is not Mc and t is not A0]
                    XTb, Eb, MLO, KBG, WN = avail
                    for q in range(4):
                        K.tr(ptv[:, 0, q, :], Mc[:, q, :], IDB[:, :])
                    K.cp("act", XTb, ptv[:, 0, :, :])
                    pse = MB
                    for q in range(4):
                        K.mm(pse[:, q * 128:(q + 1) * 128], A0[:, q, :], Mc[:, q, :])
                    K.tt("pool", Eb, bc_mid(IDB[:, :], 4), Mc, ALU.subtract)
                    K.tt("dve", Eb, Eb, pse[:, 0:512].rearrange("p (q s) -> p q s", q=4), ALU.subtract)
                    yield
                    psl = PQ
                    for q in range(4):
                        K.mm(psl[:, q * 128:(q + 1) * 128], XTb[:, q, :], Eb[:, q, :])
                    K.cp("act", MLO, psl[:, 0:512].rearrange("p (q s) -> p q s", q=4))
                    yield
                    UU, PHI, PSI = UUS[par], PHIS[par], PSIS[par]
                    for q in range(4):
                        ts_ = slice((i0 + q) * 128, (i0 + q + 1) * 128)
                        K.mm(PQ[:, q * 128:(q + 1) * 128], KTn[:, ts_], IDB[:, :])
                    K.tt("dve", KBG, PQ[:, 0:512].rearrange("p (q s) -> p q s", q=4),
                         bc_last(NBG[:, c0 + i0:c0 + i0 + 4], 128), ALU.mult)
                    yield
                    for q in range(4):
                        K.mm(PQ[:, 512 + q * 128:512 + (q + 1) * 128], Mc[:, q, :], KBG[:, q, :], start=True, stop=False)
                        K.mm(PQ[:, 512 + q * 128:512 + (q + 1) * 128], MLO[:, q, :], KBG[:, q, :], start=False, stop=True)
                    for q in range(4):
                        K.mm(MB[:, q * 128:(q + 1) * 128], Mc[:, q, :], VB[:, i0 + q, :], start=True, stop=False)
                        K.mm(MB[:, q * 128:(q + 1) * 128], MLO[:, q, :], VB[:, i0 + q, :], start=False, stop=True)
                    K.cp("act", WN, PQ[:, 512:1024].rearrange("p (q s) -> p q s", q=4))
                    K.cp("dve", UU, MB.rearrange("p (q s) -> p q s", q=4))
                    yield
                    for q in range(4):
                        K.mm(PQ[:, q * 128:(q + 1) * 128], WN[:, q, :], KD[:, i0 + q, :])
                    for q in range(4):
                        K.mm(PQ[:, 512 + q * 128:512 + (q + 1) * 128], WN[:, q, :], QKTo[:, q, :])
                    K.cp("act", PHI, PQ[:, 0:512].rearrange("p (q s) -> p q s", q=4))
                    K.tt("dve", PSI, PQ[:, 512:1024].rearrange("p (q s) -> p q s", q=4),
                         QG[:, i0 * 128:(i0 + 4) * 128].rearrange("p (q s) -> p q s", q=4), ALU.add)
                    yield

                def scan_gen(qd, par, nbase):
                    i0 = qd * 4
                    UU, PHI = UUS[par], PHIS[par]
                    order = range(4) if d == 0 else range(3, -1, -1)
                    for j, q in enumerate(order):
                        i = i0 + q
                        n_ = nbase + j
                        cc = c0 + i
                        K.mm(PH[:, 0:128], KD[:, i, :], UU[:, q, :], start=True, stop=False)
                        K.mm(PH[:, 0:128], PHI[:, q, :], SSV[n_ % 8][:, :], start=False, stop=True)
                        K.stt("dve", S32[:, :], S32[:, :], EG[:, cc:cc + 1], PH[:, 0:128], ALU.mult, ALU.add)
                        K.cp("act", SSV[(n_ + 1) % 8][:, :], S32[:, :])
                        yield

                def obatch_gen(qd, par, nbase):
                    i0 = qd * 4
                    UU, PSI, QKTo = UUS[par], PSIS[par], QKTS[par]
                    order = list(range(4)) if d == 0 else list(range(3, -1, -1))
                    for j, q in enumerate(order):
                        n_ = nbase + j
                        K.mm(PTF[:, q * 128:(q + 1) * 128], PSI[:, q, :], SSV[n_ % 8][:, :], start=True, stop=False)
                        K.mm(PTF[:, q * 128:(q + 1) * 128], QKTo[:, q, :], UU[:, q, :], start=False, stop=True)
                    ov4 = PTF[:, 0:512].rearrange("p (q s) -> p q s", q=4)
                    if d == 0:
                        K.cp("act", OACC[:, i0:i0 + 4, :], ov4)
                    else:
                        K.tt("dve", OACC[:, i0:i0 + 4, :], OACC[:, i0:i0 + 4, :], ov4, ALU.add)
                    yield

                def drain(g):
                    for _ in g:
                        pass

                def interleave(g1, g2):
                    a1 = a2 = True
                    while a1 or a2:
                        if a1:
                            try:
                                next(g1)
                            except StopIteration:
                                a1 = False
                        if a2:
                            try:
                                next(g2)
                            except StopIteration:
                                a2 = False

                invs = [inv_gen(quads[n], n % 2, n % 2) for n in range(4)]
                active = [invs[0], invs[1]]

                def step_all():
                    for g in list(active):
                        try:
                            next(g)
                        except StopIteration:
                            active.remove(g)

                while invs[0] in active:
                    step_all()
                for n, qd in enumerate(quads):
                    sc = scan_gen(qd, n % 2, 4 * n)
                    active.append(sc)
                    while sc in active:
                        step_all()
                    ob = obatch_gen(qd, n % 2, 4 * n)
                    active.append(ob)
                    while ob in active:
                        step_all()
                    if n + 2 < 4:
                        active.append(invs[n + 2])
                    while n + 1 < 4 and invs[n + 1] in active:
                        step_all()
            if l == 0:
                dbg_store(f"oacc{h}", FB[0][:, :])
            gps = [PS[0], PS[1]]
            for half in range(2):
                projT(gwb[:, :, 0:128], gps[half], half * 1024, 1024)
            SQ = FB[1].rearrange("p (i d) -> p i d", i=NT)
            K.act(SQ, OACC, AF.Square)
            K.P.add("dve", lambda e: e.tensor_reduce(out=STAT[:, 48:64], in_=SQ, axis=AX.X, op=ALU.add),
                    reads=[SQ], writes=[STAT[:, 48:64]])
            K.ts("dve", STAT[:, 48:64], STAT[:, 48:64], 1.0 / 128, EPS, ALU.mult, ALU.add)
            K.act(STAT[:, 48:64], STAT[:, 48:64], AF.Sqrt)
            K.recip(STAT[:, 48:64], STAT[:, 48:64])
            ON = KD
            K.tt("dve", ON, OACC, bc_last(STAT[:, 48:64], 128), ALU.mult)
            for half in range(2):
                hs = slice(half * 1024, (half + 1) * 1024)
                K.act(FB[1][:, hs], gps[half][:, :], AF.Silu)
            for g8 in range(2):
                pbank = PT[:, :] if g8 == 0 else PH_[:, :].bitcast(BF16)
                ptv8 = pbank.rearrange("p (q t) -> p q t", q=8)
                for q in range(8):
                    K.tr(ptv8[:, q, :], ON[:, g8 * 8 + q, :], IDB[:, :])
                K.stt("dve", Y[:, 2 + h, g8 * 1024:(g8 + 1) * 1024], pbank, PPT[:, 44:45],
                      FB[1][:, g8 * 1024:(g8 + 1) * 1024], ALU.mult, ALU.mult)


    def mixer_C(l, YC):
        w2 = w_in_d[l]
        QT = carve(HBO, [128, T], BF16)
        KT = carve(HBO + 4096, [128, T], BF16)
        ACC = carve(HBO + 8192, [128, 2, T], F32)
        VA = carve(HBO + 24576, [128, 16, 2, 66], BF16)
        PSB = [carve(FBO + 16384 + i * 1024, [128, 2, 2, 128], BF16) for i in range(2)]
        XB = carve(FBO, [128, T], BF16)
        PSB2 = [carve(FBO + 16384 + i * 1024, [128, 2, 256], BF16) for i in range(2)]
        AMASK2 = carve(HBO + 29056, [128, 256], BF16)
        K.cp("pool", AMASK2[:, 0:128], AMASK[:, 1, :])
        K.cp("pool", AMASK2[:, 128:256], AMASK[:, 0, :])
        RMB = carve(HBO + 28800, [128, 128], BF16)
        K.cp("dve", RMB, CST[:, 128:256])
        pn = 0
        for hp in range(2):
            K.memset("pool", ACC[0:65, :, :], 0.0)
            for p, dil in enumerate((1, 4, 16)):
                if p not in flags.get('pats', (0, 1, 2)):
                    continue
                L = T // dil
                nj = L // 128
                for which, dst in ((C_QC, QT), (C_KC, KT)):
                    wb = next_wb()
                    load_w(wb[:, :, 0:128], w2, which + p * 256 + hp * 128, 128)
                    for half in range(2):
                        hs = slice(half * 1024, (half + 1) * 1024)
                        ps = next_ps()
                        projT(wb[:, :, 0:128], ps, half * 1024, 1024)
                        K.cp("act", XB[:, hs], ps[:, :])
                    K.tt("dve", FB[1][:, :], XB[:, :], COS[:, :], ALU.mult)
                    for half in range(2):
                        hs = slice(half * 1024, (half + 1) * 1024)
                        ps = next_ps()
                        for b in range(2):
                            K.mm(ps[:, b * 512:(b + 1) * 512], RMB, XB[:, half * 1024 + b * 512:half * 1024 + (b + 1) * 512])
                        K.tt("dve", FB[2][:, hs], ps[:, :], SINS[:, hs], ALU.mult)
                    K.tt("dve", dst[:, :], FB[1][:, :], FB[2][:, :], ALU.add)
                if dbg is not None and l == 0 and hp == 0:
                    dbg_store(f"qt{p}", QT[:, :])
                    dbg_store(f"kt{p}", KT[:, :])
                if flags.get('cst', 9) < 2:
                    continue
                wb = next_wb()
                load_w(wb[:, :, 0:128], w2, C_VC + p * 256 + hp * 128, 128)
                K.memset("pool", VA[:, :, :, 64:65], 1.0)
                for g4 in range(4):
                    ps = next_ps()
                    for q in range(4):
                        idx = g4 * 4 + q
                        r, j = idx // nj, idx % nj
                        st = (128 * j) * dil + r
                        for kc in range(8):
                            K.mm(ps[:, q * 128:(q + 1) * 128], HT[:, kc, st:st + 127 * dil + 1:dil], wb[:, kc, 0:128],
                                 start=(kc == 0), stop=(kc == 7))
                    K.cp("act", VA[:, g4 * 4:(g4 + 1) * 4, :, 0:64],
                         ps[:, 0:512].rearrange("p (q h d) -> p q h d", q=4, h=2))
                if flags.get('cst', 9) < 3:
                    continue
                tiles = [(r, j) for r in range(dil) for j in range(nj)]
                pend = []

                def stage_a(r, j):
                    nonlocal pn
                    qlo, qhi = max(0, 128 * j - 64), min(L, 128 * j + 192)
                    nq = qhi - qlo
                    qoff = qlo - (128 * j - 64)
                    qsl = slice(qlo * dil + r, (qhi - 1) * dil + r + 1, dil)
                    ksl = slice(128 * j * dil + r, (128 * j + 127) * dil + r + 1, dil)
                    ps = next_ps()
                    pv3 = ps[:, :].rearrange("p (h c) -> p h c", h=2)
                    for hh in range(2):
                        K.mm(pv3[:, hh, qoff:qoff + nq], KT[hh * 64:(hh + 1) * 64, ksl], QT[hh * 64:(hh + 1) * 64, qsl])
                    pb = PSB2[pn % 2]
                    pn += 1
                    K.act(pb[:, :, qoff:qoff + nq], pv3[:, :, qoff:qoff + nq], AF.Exp, scale=0.125)
                    K.tt("dve", pb[:, :, qoff:qoff + nq], pb[:, :, qoff:qoff + nq],
                         bc_mid(AMASK2[:, qoff:qoff + nq], 2), ALU.mult)
                    return (r, j, nq, qoff, qsl, pv3, pb)

                def stage_b(st):
                    r, j, nq, qoff, qsl, pv3, pb = st
                    for hh in range(2):
                        K.mm(pv3[0:65, hh, 256 + qoff:256 + qoff + nq], VA[:, r * nj + j, hh, 0:65], pb[:, hh, qoff:qoff + nq])
                    K.tt("dve", ACC[0:65, :, qsl], ACC[0:65, :, qsl], pv3[0:65, :, 256 + qoff:256 + qoff + nq], ALU.add)

                for (r, j) in tiles:
                    pend.append(stage_a(r, j))
                    if len(pend) > 1:
                        stage_b(pend.pop(0))
                while pend:
                    stage_b(pend.pop(0))
            if flags.get('cst', 9) < 4:
                continue
            for hh in range(2):
                h = hp * 2 + hh
                K.act(FB[0][0:64, :], ACC[0:64, hh, :], AF.Square)
                for half in range(2):
                    hs = slice(half * 1024, (half + 1) * 1024)
                    ps = next_ps()
                    for b in range(2):
                        K.mm(ps[0:64, b * 512:(b + 1) * 512], SEL[0:65, :], ACC[0:65, hh, half * 1024 + b * 512:half * 1024 + (b + 1) * 512])
                    K.act(FB[1][0:64, hs], ps[0:64, :], AF.Square, scale=float(np.sqrt(EPS)))
                    ps2 = next_ps()
                    for b in range(2):
                        K.mm(ps2[0:64, b * 512:(b + 1) * 512], ONF[0:64, 0:64], FB[0][0:64, half * 1024 + b * 512:half * 1024 + (b + 1) * 512])
                    K.stt("dve", FB[1][0:64, hs], ps2[0:64, :], 1.0 / 64, FB[1][0:64, hs], ALU.mult, ALU.add)
                K.act(FB[1][0:64, :], FB[1][0:64, :], AF.Ln)
                K.act(FB[1][0:64, :], FB[1][0:64, :], AF.Exp, scale=-0.5)
                K.stt("dve", YC[0:64, h, :], ACC[0:64, hh, :], PPT[0:64, 45 + h:46 + h], FB[1][0:64, :], ALU.mult, ALU.mult)

    for l in range(DEPTH):
        K.dma("sp", PPT[:, :], pp_d[l])
        K.dma("sp", BCT[:, :], bc_d[l])
        if flags.get("A") or flags.get("B") or flags.get("C"):
            rmsnorm_to_HT(norm_mix_d[l, :])
        if flags.get("C"):
            YC = carve(RO, [128, 4, T], BF16)
            mixer_C(l, YC)
            dbg_store(f"yc{l}", YC[0:64, :, :])
            wo_pass(l, lambda c, i: YC[0:64, c, i * 128:(i + 1) * 128], 4, 64, 768)
        if flags.get("A") or flags.get("B"):
            Y = carve(RO, [128, 6, T], BF16)
            if flags.get("B"):
                mixer_B(l, Y)
            else:
                K.memset("pool", Y[:, 2:6, :], 0.0)
            if flags.get("A"):
                mixer_A(l, Y)
            else:
                K.memset("pool", Y[:, 0:2, :], 0.0)
            dbg_store(f"y{l}", Y[:, :, :])
            wo_pass(l, lambda c, i: Y[:, c, i * 128:(i + 1) * 128], 6, 128, 0)
        if flags.get("FFN"):
            ffn(l)

    K.dma("sp", GBC[:, :], norm_final_d[0, :].partition_broadcast(128))
    junk = carve(FBO, [128, D], BF16)
    junk2 = carve(FBO + 2048, [128, D], BF16)
    for i in range(NT):
        if i % 2 == 0:
            K.act(junk, XT[:, i, :], AF.Square, accum=STAT[:, i:i + 1])
        else:
            K.P.add("dve", lambda e, i=i: e.scalar_tensor_tensor(out=junk2, in0=XT[:, i, :], scalar=1.0, in1=XT[:, i, :],
                                                                  op0=ALU.mult, op1=ALU.mult, accum_out=STAT[:, i:i + 1]),
                    reads=[XT[:, i, :]], writes=[junk2, STAT[:, i:i + 1]])
    K.ts("dve", STAT[:, 16:32], STAT[:, 0:16], 1.0 / D, EPS, ALU.mult, ALU.add)
    K.act(STAT[:, 16:32], STAT[:, 16:32], AF.Sqrt)
    K.recip(STAT[:, 32:48], STAT[:, 16:32])
    ob = [carve(FBO + 8192, [128, D], F32), carve(FBO + 8192 + 4096, [128, D], F32)]
    ov = out_d.rearrange("(n p) d -> p n d", p=128)
    for i in range(NT):
        o = ob[i % 2]
        K.stt("dve", o, XT[:, i, :], STAT[:, 32 + i:33 + i], GBC[:, :], ALU.mult, ALU.mult)
        K.dma("sp", ov[:, i, :], o)

    n = K.P.emit_all()
    print("instructions:", n)
    return nc, dbg_out


def make_consts():
    cst = np.zeros((128, 512), np.float32)
    inv = (np.float32(500000.0) ** (-np.arange(8, dtype=np.float32) / np.float32(8))).astype(np.float32)
    for p in range(128):
        j = p % 64
        if j < 16:
            cst[p, 0] = inv[j % 8]
            cst[p, 1] = -1.0 if j < 8 else 1.0
            src = p + 8 if j < 8 else p - 8
            cst[src, 128 + p] = 1.0
    return cst


def prep_inputs(inputs):
    f = lambda a: np.ascontiguousarray(np.asarray(a))
    x = f(inputs["x"])
    pos = f(inputs["positions"]).astype(np.int32)
    pp = np.zeros((DEPTH, 128, NPP), np.float32)
    conv_a = f(inputs["conv_a"]); norm_a = f(inputs["norm_a"]); conv_qkv = f(inputs["conv_qkv"])
    norm_dn = f(inputs["norm_dn"]); norm_c = f(inputs["norm_c"]); conv_ffn = f(inputs["conv_ffn"])
    for l in range(DEPTH):
        pp[l, :, 0:6] = conv_a[l].reshape(3, 2, 128).transpose(2, 1, 0).reshape(128, 6)
        pp[l, :, 6:8] = norm_a[l].reshape(2, 128).T
        pp[l, :, 8:44] = conv_qkv[l].reshape(3, 12, 128).transpose(2, 1, 0).reshape(128, 36)
        pp[l, :, 44] = norm_dn[l]
        pp[l, 0:64, 45:49] = norm_c[l].reshape(4, 64).T
        pp[l, :, 60:60 + 132] = conv_ffn[l].reshape(3, 44, 128).transpose(2, 1, 0).reshape(128, 132)
    bc = np.zeros((DEPTH, 128, 16), np.float32)
    for l in range(DEPTH):
        row = np.concatenate([f(inputs["a_log_f"])[l], f(inputs["a_log_b"])[l],
                              f(inputs["dt_bias_f"])[l], f(inputs["dt_bias_b"])[l]])
        bc[l] = np.broadcast_to(row[None, :], (128, 16))
    w_up_r = np.ascontiguousarray(
        f(inputs["w_up"]).reshape(DEPTH, 8, 128, 2, NFT, 128).transpose(0, 4, 2, 1, 3, 5)).reshape(DEPTH, NFT, 128, 2048)
    shared = dict(norm_mix=f(inputs["norm_mix"]), w_in=f(inputs["w_in"]), w_o=f(inputs["w_o"]),
                  norm_ffn=f(inputs["norm_ffn"]), w_up=w_up_r, w_down=f(inputs["w_down"]),
                  norm_final=f(inputs["norm_final"]).reshape(1, D), pp=pp, bc=bc, cst=make_consts())
    maps = []
    for b in range(x.shape[0]):
        m = dict(shared)
        m["x"] = x[b]
        m["pos"] = pos[b].reshape(1, T)
        maps.append(m)
    return maps


def kernel(**inputs):
    maps = prep_inputs(inputs)
    nc, _ = build()
    res = run_bass_kernel_spmd(nc, maps, core_ids=list(range(len(maps))))
    return np.stack([r["out"] for r in res.results], axis=0).astype(np.float32)
```
